# Optimizing a Trainium2 kernel written in Bass

```python
import math
import jax, jax.numpy as jnp
from jax import lax
import numpy as np

D_MODEL = 1024
BATCH = 32
SEQ = 256
DEPTH = 2
DEC_BATCH = 8
DEC_SEQ = 4096
PAST_LEN = 512

GRID_W = 64
N_EVEN = (DEPTH + 1) // 2
N_ODD = DEPTH // 2
RET_HEADS = 4
RET_DK = 128
RET_DV = 128
RET_WIDTH = RET_HEADS * RET_DV
RET_CHUNK = 128
ROPE_BASE = 10000.0
CONV_WIDTH = D_MODEL // 2
CONV_K = 3
AB_IN = 3 * RET_HEADS * RET_DK + RET_WIDTH + 3 * CONV_WIDTH
AB_IN = 2 * RET_HEADS * RET_DK + 2 * RET_WIDTH + 3 * CONV_WIDTH
AB_OUT = RET_WIDTH + CONV_WIDTH
CMLP_WIDTH = D_MODEL
CMLP_GROUPS = 4
CMLP_CHUNK = 128
FFN_HIDDEN = ((8 * D_MODEL // 3 + 255) // 256) * 256
EPS = 1e-6

kernel_name = 'hybrid_retention_conv_chunkmlp_diffusion_step'


def _rmsnorm(x, g):
    x32 = x.astype(jnp.float32)
    y = x32 * lax.rsqrt(jnp.mean(x32 * x32, axis=-1, keepdims=True) + EPS)
    return (y * g.astype(jnp.float32)).astype(x.dtype)


def _rope_2d(T):
    rows = T // GRID_W
    row = jnp.repeat(jnp.arange(rows, dtype=jnp.float32), GRID_W)
    col = jnp.tile(jnp.arange(GRID_W, dtype=jnp.float32), rows)
    nf = RET_DK // 4
    freqs = ROPE_BASE ** (-jnp.arange(nf, dtype=jnp.float32) / nf)
    ang = jnp.concatenate([row[:, None] * freqs, col[:, None] * freqs], axis=-1)
    return jnp.cos(ang)[:, None, :], jnp.sin(ang)[:, None, :]


def _apply_rope(x, cos, sin):
    half = x.shape[-1] // 2
    x1, x2 = x[..., :half], x[..., half:]
    cos = cos.astype(x.dtype)
    sin = sin.astype(x.dtype)
    return jnp.concatenate([x1 * cos - x2 * sin, x2 * cos + x1 * sin], axis=-1)


def _retention_scan(q, k, v, log_gamma, S0):
    B, T, H, DK = q.shape
    DV = v.shape[-1]
    n = T // RET_CHUNK
    dt = q.dtype

    def to_chunks(a):
        return a.reshape(B, n, RET_CHUNK, H, a.shape[-1]).transpose(1, 0, 3, 2, 4)

    qc, kc, vc = to_chunks(q), to_chunks(k), to_chunks(v)
    idx = jnp.arange(RET_CHUNK, dtype=jnp.float32)
    diff = idx[:, None] - idx[None, :]
    lg = log_gamma[:, None, None]
    dmask = jnp.where(diff >= 0, jnp.exp(jnp.maximum(diff, 0.0) * lg), 0.0).astype(dt)
    cross = jnp.exp((idx + 1.0) * log_gamma[:, None]).astype(dt)[..., None]
    kdec = jnp.exp((RET_CHUNK - 1.0 - idx) * log_gamma[:, None]).astype(dt)[..., None]
    chunk_dec = jnp.exp(RET_CHUNK * log_gamma).astype(dt)[:, None, None]

    def step(S, inp):
        qb, kb, vb = inp
        scores = jnp.einsum('bhid,bhjd->bhij', qb, kb) * dmask
        o = jnp.einsum('bhij,bhjv->bhiv', scores, vb) + jnp.einsum('bhid,bhdv->bhiv', qb, S) * cross
        S_new = S * chunk_dec + jnp.einsum('bhjd,bhjv->bhdv', kb * kdec, vb)
        return S_new, o

    S_fin, o = lax.scan(step, S0.astype(dt), (qc, kc, vc))
    o = o.transpose(1, 0, 3, 2, 4).reshape(B, T, H, DV)
    return o, S_fin


def _short_conv(x, w, b):
    xp = jnp.pad(x, ((0, 0), (1, 1), (0, 0)))
    return w[0] * xp[:, :-2] + w[1] * xp[:, 1:-1] + w[2] * xp[:, 2:] + b


def _mixer_ab(h, w_in, decay_logit, ret_g, conv_w, conv_b, w_out, S0, use_rope):
    B, T, _ = h.shape
    p = h @ w_in
    qd = RET_HEADS * RET_DK
    splits = np.cumsum([qd, qd, RET_WIDTH, RET_WIDTH, CONV_WIDTH, CONV_WIDTH])
    q, k, v, g, bg, cg, xc = jnp.split(p, splits, axis=-1)
    q = q.reshape(B, T, RET_HEADS, RET_DK) * (RET_DK ** -0.5)
    k = k.reshape(B, T, RET_HEADS, RET_DK)
    v = v.reshape(B, T, RET_HEADS, RET_DV)
    if use_rope:
        cos, sin = _rope_2d(T)
        q = _apply_rope(q, cos, sin)
        k = _apply_rope(k, cos, sin)
    lg = jax.nn.log_sigmoid(decay_logit.astype(jnp.float32))
    o_f, S_f = _retention_scan(q, k, v, lg[0], S0[:, 0])
    o_b, S_b = _retention_scan(q[:, ::-1], k[:, ::-1], v[:, ::-1], lg[1], S0[:, 1])
    o = o_f + o_b[:, ::-1]
    o = _rmsnorm(o, jnp.ones((RET_DV,), jnp.float32)).reshape(B, T, RET_WIDTH) * ret_g
    ya = jax.nn.silu(g) * o
    yb = bg * _short_conv(cg * xc, conv_w, conv_b)
    out = jnp.concatenate([ya, yb], axis=-1) @ w_out
    S_fin = jnp.stack([S_f, S_b], axis=1)
    return out, S_fin


def _mixer_c(h, w_in, v_g, w_s, b_s, w_out):
    B, T, _ = h.shape
    z = jax.nn.gelu(h @ w_in)
    u, v = jnp.split(z, 2, axis=-1)
    v = _rmsnorm(v, v_g)
    n = T // CMLP_CHUNK
    v = v.reshape(B, n, CMLP_CHUNK, CMLP_GROUPS, CMLP_WIDTH // CMLP_GROUPS)
    s = jnp.einsum('gpq,bnqgc->bnpgc', w_s, v) + b_s.T[None, None, :, :, None]
    return (u * s.reshape(B, T, CMLP_WIDTH)) @ w_out


def _swiglu(h, w_gate, w_up, w_down):
    return (jax.nn.silu(h @ w_gate) * (h @ w_up)) @ w_down


def _trunk(x, cvec, ret_init, use_rope, ada_w, ada_b, norm_mix_g, norm_ffn_g, w_in_ab, ret_decay_logit,
           ret_norm_g, conv_w, conv_b, w_out_ab, w_in_c, c_norm_g, w_spatial, b_spatial, w_out_c,
           w_gate, w_up, w_down, final_norm_g):
    states = []
    sc = jax.nn.silu(cvec)
    for l in range(DEPTH):
        mod = sc @ ada_w[l] + ada_b[l]
        sh1, sc1, g1, sh2, sc2, g2 = [m[:, None, :] for m in jnp.split(mod, 6, axis=-1)]
        h = _rmsnorm(x, norm_mix_g[l]) * (1.0 + sc1) + sh1
        if l % 2 == 0:
            i = l // 2
            out, S_fin = _mixer_ab(h, w_in_ab[i], ret_decay_logit[i], ret_norm_g[i], conv_w[i], conv_b[i],
                                   w_out_ab[i], ret_init[:, i], use_rope)
            states.append(S_fin)
        else:
            i = l // 2
            out = _mixer_c(h, w_in_c[i], c_norm_g[i], w_spatial[i], b_spatial[i], w_out_c[i])
        x = x + g1 * out
        h = _rmsnorm(x, norm_ffn_g[l]) * (1.0 + sc2) + sh2
        x = x + g2 * _swiglu(h, w_gate[l], w_up[l], w_down[l])
    return _rmsnorm(x, final_norm_g), jnp.stack(states, axis=1)


def setup_inputs(seed: int = 0) -> dict:
    key = jax.random.key(seed)
    ks = jax.random.split(key, 24)
    f32 = jnp.float32
    nrm = lambda k, shape, s: jax.random.normal(k, shape, f32) * s
    gamma = 1.0 - 2.0 ** (-5.0 - np.arange(RET_HEADS))
    base_logit = jnp.asarray(np.log(gamma / (1.0 - gamma)), f32)
    return {
        'x_prompt': nrm(ks[0], (BATCH, SEQ, D_MODEL), 1.0),
        'x_sample': nrm(ks[1], (DEC_BATCH, DEC_SEQ, D_MODEL), 1.0),
        'state_ret': nrm(ks[2], (DEC_BATCH, N_EVEN, 2, RET_HEADS, RET_DK, RET_DV), 0.5),
        'c': nrm(ks[3], (DEC_BATCH, D_MODEL), 1.0),
        'c_ctx': nrm(ks[4], (D_MODEL,), 1.0),
        'ada_w': nrm(ks[5], (DEPTH, D_MODEL, 6 * D_MODEL), 0.5 * D_MODEL ** -0.5),
        'ada_b': nrm(ks[6], (DEPTH, 6 * D_MODEL), 0.02),
        'norm_mix_g': 1.0 + nrm(ks[7], (DEPTH, D_MODEL), 0.02),
        'norm_ffn_g': 1.0 + nrm(ks[8], (DEPTH, D_MODEL), 0.02),
        'w_in_ab': nrm(ks[9], (N_EVEN, D_MODEL, AB_IN), D_MODEL ** -0.5),
        'ret_decay_logit': base_logit + nrm(ks[10], (N_EVEN, 2, RET_HEADS), 0.1),
        'ret_norm_g': 1.0 + nrm(ks[11], (N_EVEN, RET_WIDTH), 0.02),
        'conv_w': nrm(ks[12], (N_EVEN, CONV_K, CONV_WIDTH), CONV_K ** -0.5),
        'conv_b': nrm(ks[13], (N_EVEN, CONV_WIDTH), 0.02),
        'w_out_ab': nrm(ks[14], (N_EVEN, AB_OUT, D_MODEL), AB_OUT ** -0.5),
        'w_in_c': nrm(ks[15], (N_ODD, D_MODEL, 2 * CMLP_WIDTH), D_MODEL ** -0.5),
        'c_norm_g': 1.0 + nrm(ks[16], (N_ODD, CMLP_WIDTH), 0.02),
        'w_spatial': nrm(ks[17], (N_ODD, CMLP_GROUPS, CMLP_CHUNK, CMLP_CHUNK), CMLP_CHUNK ** -0.5),
        'b_spatial': 1.0 + nrm(ks[18], (N_ODD, CMLP_GROUPS, CMLP_CHUNK), 0.02),
        'w_out_c': nrm(ks[19], (N_ODD, CMLP_WIDTH, D_MODEL), CMLP_WIDTH ** -0.5),
        'w_gate': nrm(ks[20], (DEPTH, D_MODEL, FFN_HIDDEN), D_MODEL ** -0.5),
        'w_up': nrm(ks[21], (DEPTH, D_MODEL, FFN_HIDDEN), D_MODEL ** -0.5),
        'w_down': nrm(ks[22], (DEPTH, FFN_HIDDEN, D_MODEL), FFN_HIDDEN ** -0.5),
        'final_norm_g': 1.0 + nrm(ks[23], (D_MODEL,), 0.02),
    }


def reference(x_prompt, x_sample, state_ret, c, c_ctx, ada_w, ada_b, norm_mix_g, norm_ffn_g, w_in_ab,
              ret_decay_logit, ret_norm_g, conv_w, conv_b, w_out_ab, w_in_c, c_norm_g, w_spatial,
              b_spatial, w_out_c, w_gate, w_up, w_down, final_norm_g):
    weights = (ada_w, ada_b, norm_mix_g, norm_ffn_g, w_in_ab, ret_decay_logit, ret_norm_g, conv_w, conv_b,
               w_out_ab, w_in_c, c_norm_g, w_spatial, b_spatial, w_out_c, w_gate, w_up, w_down, final_norm_g)
    ctx_init = jnp.zeros((x_prompt.shape[0], N_EVEN, 2, RET_HEADS, RET_DK, RET_DV), x_prompt.dtype)
    y_prompt, new_state_ret = _trunk(x_prompt, c_ctx[None, :], ctx_init, False, *weights)
    y_sample, _ = _trunk(x_sample, c, state_ret, True, *weights)
    return (y_prompt, y_sample, new_state_ret)
```

```python
import math
import numpy as np
import concourse.bass as bass
import concourse.mybir as mybir
from concourse.bass_utils import run_bass_kernel_spmd

F32 = mybir.dt.float32
BF16 = mybir.dt.bfloat16
AF = mybir.ActivationFunctionType
ALU = mybir.AluOpType

COMPUTE = ("pe", "act", "dve", "pool")
ENGS = ("pe", "act", "dve", "pool", "sp")
SEM_WRAP = 30000
BLK = 256
EPS = 1e-6
QSCALE_LN = math.log(128.0 ** -0.5)


class StopBuild(Exception):
    pass


class Op:
    __slots__ = ("eng", "fn", "idx", "seq", "waits", "signaled", "sigval", "chan", "chanval", "known", "tag")


class Sched:
    def __init__(self):
        self.ops = {e: [] for e in ENGS}
        self.recs = {}
        self.chan_tot = {}
        self.seq = 0
        self.tag = ''

    def add(self, eng, fn, reads=(), writes=(), chan=None):
        o = Op()
        o.eng = eng
        o.fn = fn
        o.chan = chan
        o.signaled = False
        o.sigval = 0
        o.tag = self.tag
        lst = self.ops[eng]
        o.idx = len(lst)
        o.seq = self.seq
        self.seq += 1
        deps = {}
        rkey = chan if chan is not None else eng
        recs = self.recs
        psr = [r for r in reads if r[0][0] == "p"]
        if psr:
            writes = list(writes) + psr
        for (key, i0, i1) in reads:
            for i in range(i0, i1):
                r = recs.get((key, i))
                if r is not None and r[0] is not None:
                    d = r[0]
                    deps[d.seq] = d
        for (key, i0, i1) in writes:
            for i in range(i0, i1):
                r = recs.get((key, i))
                if r is not None:
                    d = r[0]
                    if d is not None and not (d.chan is None and chan is None and d.eng == eng):
                        deps[d.seq] = d
                    for d in r[1].values():
                        if not (d.chan is None and chan is None and d.eng == eng):
                            deps[d.seq] = d
        for (key, i0, i1) in reads:
            for i in range(i0, i1):
                r = recs.get((key, i))
                if r is None:
                    r = [None, {}]
                    recs[(key, i)] = r
                r[1][rkey] = o
        for (key, i0, i1) in writes:
            for i in range(i0, i1):
                recs[(key, i)] = [o, {}]
        known = dict(lst[-1].known) if lst else {}
        waits = []
        for sq in sorted(deps.keys(), reverse=True):
            d = deps[sq]
            if d.chan is None:
                if d.eng == "pe" and eng == "pe" and chan is None:
                    continue
                k, v = d.eng, d.idx
            else:
                k, v = d.chan, d.chanval
            if known.get(k, -1) >= v:
                continue
            if d.chan is not None:
                v = self.chan_tot[d.chan]
            else:
                d.signaled = True
            waits.append((k, v, d))
            for kk, vv in d.known.items():
                if known.get(kk, -1) < vv:
                    known[kk] = vv
            if known.get(k, -1) < v:
                known[k] = v
        o.known = known
        o.waits = waits
        if chan is not None:
            self.chan_tot[chan] = self.chan_tot.get(chan, 0) + 1
            o.chanval = self.chan_tot[chan]
        lst.append(o)
        return o

    def emit(self, nc, final_chans=()):
        nsig = {}
        for e in COMPUTE:
            c = 0
            for o in self.ops[e]:
                if o.signaled:
                    c += 1
                    o.sigval = c
            nsig[e] = c
        sems = {}
        for e in COMPUTE:
            n = max(1, (nsig[e] + SEM_WRAP - 1) // SEM_WRAP)
            sems[e] = [nc.alloc_semaphore("s_%s_%d" % (e, i)) for i in range(n)]
        csems = {c: nc.alloc_semaphore("c_%s" % c) for c in self.chan_tot}
        ops = self.ops
        chan_tot = self.chan_tot

        def run(name, eng):
            for o in ops[name]:
                for (k, v, d) in o.waits:
                    if d.chan is None:
                        sv = d.sigval
                        eng.wait_ge(sems[k][(sv - 1) // SEM_WRAP], (sv - 1) % SEM_WRAP + 1)
                    else:
                        eng.wait_ge(csems[k], 16 * v)
                ins = o.fn(eng)
                if o.chan is not None:
                    ins.then_inc(csems[o.chan], 16)
                elif o.signaled:
                    ins.then_inc(sems[name][(o.sigval - 1) // SEM_WRAP], 1)
            if name == "sp":
                for c in chan_tot:
                    eng.wait_ge(csems[c], 16 * chan_tot[c])

        with nc.Block() as block:
            @block.tensor
            def _(e):
                run("pe", e)

            @block.scalar
            def _(e):
                run("act", e)

            @block.vector
            def _(e):
                run("dve", e)

            @block.gpsimd
            def _(e):
                run("pool", e)

            @block.sync
            def _(e):
                run("sp", e)
        return {e: (len(self.ops[e]), nsig.get(e, 0)) for e in ENGS}


class AB:
    def __init__(self, SB, off, n0, inner, dtype):
        es = 2 if dtype == BF16 else 4
        inner = list(inner)
        self.slice_bytes = es * int(np.prod(inner))
        self.off = off
        self.n0 = n0
        self.size = self.slice_bytes * n0
        assert off % 4 == 0 and self.size % 4 == 0
        ap = SB[:, off // 4:(off + self.size) // 4]
        if dtype == BF16:
            ap = ap.bitcast(BF16)
        if len(inner) == 1:
            ap = ap.rearrange("p (a b) -> p a b", a=n0)
        elif len(inner) == 2:
            ap = ap.rearrange("p (a b c) -> p a b c", a=n0, b=inner[0])
        else:
            raise ValueError
        self.ap = ap

    def __getitem__(self, k):
        return self.ap[k]

    def r(self, i0=None, i1=None):
        if i0 is None:
            i0, i1 = 0, self.n0
        elif i1 is None:
            i1 = i0 + 1
        lo = self.off + i0 * self.slice_bytes
        hi = self.off + i1 * self.slice_bytes
        return ("sb", lo // BLK, (hi + BLK - 1) // BLK)


class DB:
    def __init__(self, t, name, n0):
        self.t = t
        self.name = name
        self.n0 = n0

    def r(self, i0=None, i1=None):
        if i0 is None:
            return (self.name, 0, self.n0)
        if i1 is None:
            i1 = i0 + 1
        return (self.name, i0, i1)


def weight_stream():
    st = []
    for grp in range(3):
        for kc in range(8):
            for j in range(4):
                st.append(("w_in_ab", 0, kc, grp * 4 + j))
    for grp in (3, 5, 6, 4):
        for mc in range(4):
            for kc in range(8):
                st.append(("w_in_ab", 0, kc, grp * 4 + mc))
    for mc in range(8):
        for ac in range(8):
            st.append(("w_out_ab", 0, ac, mc))

    def ffn(l):
        for m in range(22):
            for kc in range(8):
                for mat in ("w_gate", "w_up"):
                    st.append((mat, l, kc, m))
        for mc in range(8):
            for kc in range(22):
                st.append(("w_down", l, kc, mc))
    ffn(0)
    for half in range(2):
        for kc in range(8):
            for j in range(4):
                st.append(("w_in_c", 0, kc, 8 + half * 4 + j))
    for mc in range(8):
        for kc in range(8):
            st.append(("w_in_c", 0, kc, mc))
    for mc in range(8):
        for kc in range(8):
            st.append(("w_out_c", 0, kc, mc))
    ffn(1)
    assert len(st) == 1536
    return st


NUNIT = 48
V_ADAB, V_C, V_CCTX, V_NMG, V_NFG, V_RETG, V_CONVW, V_CONVB, V_FING = 0, 96, 104, 112, 128, 144, 148, 160, 164
NV = 172
R_CNG, R_BS, R_DL = 0, 1024, 1536
NR = 1544
C_ID, C_NA, C_NB, C_M1, C_M2, C_NI1, C_NI2, C_PC = 0, 128, 256, 384, 512, 640, 768, 896
NCST = 898


def build(NS, NP, stage=9):
    NT = NS + NP
    nc = bass.Bass("TRN2", target_bir_lowering=False)
    S = Sched()
    R = 4

    def din(name, shape, dt=F32):
        return nc.dram_tensor(name, list(shape), dt, kind="ExternalInput")

    def dout(name, shape):
        return nc.dram_tensor(name, list(shape), F32, kind="ExternalOutput")

    xs = din("xs", [max(NS, 1) * 512, 1024])
    xp = din("xp", [max(NP, 1) * 512, 1024])
    st0 = din("st0", [2, 4, 128, 128])
    wall = din("wall", [NUNIT, 128, 4096])
    adaw = din("adaw", [2, 1024, 6144])
    vecT_d = din("vecT", [128, NV])
    rowv_d = din("rowv", [NR])
    wsT_d = din("wsT", [128, 512])
    cst_d = din("cst", [128, NCST])
    rope_d = din("ropeT", [max(NS, 1), 128, 3, 4, 64])
    ys = dout("ys", [max(NS, 1) * 512, 1024])
    yp = dout("yp", [max(NP, 1) * 512, 1024])
    nst = dout("nst", [max(2 * NP, 1), 2, 4, 128, 128])
    wbf = DB(nc.dram_tensor("wbf", [NUNIT, 128, 4096], BF16, kind="Internal"), "wbf", NUNIT)
    sbscr = DB(nc.dram_tensor("sbscr", [NT, 128, 4, 512], BF16, kind="Internal"), "sbscr", NT)
    kvscr = DB(nc.dram_tensor("kvscr", [NT, 128, 8, 512], BF16, kind="Internal"), "kvscr", NT)

    TOTAL = 207 * 1024
    SB = nc.alloc_sbuf_tensor("SB", [128, TOTAL // 4], F32)
    cur = [0]

    def alloc(n0, inner, dt):
        b = AB(SB, cur[0], n0, inner, dt)
        cur[0] += (b.size + BLK - 1) // BLK * BLK
        return b

    ring = alloc(R, [4096], BF16)
    xin = alloc(4, [1024], F32)
    ost = alloc(2, [1024], F32)
    xT = alloc(8, [512], F32)
    hT = alloc(8, [512], BF16)
    tmpf = alloc(4, [512], F32)
    sqb = alloc(4, [512], BF16)
    rbuf = alloc(2, [512], F32)
    identF = alloc(1, [128], F32)
    vec = alloc(1, [NV], F32)
    rows = alloc(1, [NR], F32)
    idb = alloc(1, [128], BF16)
    onesb = alloc(1, [128], BF16)
    wsTb = alloc(1, [512], BF16)
    nlg = alloc(1, [8], F32)
    ecol = alloc(1, [8], F32)
    DT = alloc(4, [128], F32)
    dtmp = alloc(1, [128], F32)
    crossf = alloc(1, [512], F32)
    crossb = alloc(1, [512], F32)
    kdecf = alloc(1, [512], F32)
    kdecb = alloc(1, [512], F32)
    cdecf = alloc(1, [512], F32)
    cdecb = alloc(1, [512], F32)
    kcol = alloc(1, [24], F32)
    scT = alloc(1, [8, 2], BF16)
    modT = alloc(1, [192], F32)
    Gt = alloc(1, [64], F32)
    ropeb = alloc(2, [12, 64], F32)
    Srun = [alloc(1, [512], F32), alloc(1, [512], F32)]
    scur = [0]
    zh = alloc(1, [NT, 8], F32)
    persist_end = cur[0]
    qkf = alloc(1, [512], F32)
    rtb = alloc(1, [512], F32)
    q_tm = alloc(4, [512], BF16)
    k_tm = alloc(4, [512], BF16)
    kdf = alloc(4, [512], BF16)
    v_tm = alloc(4, [512], BF16)
    qT = alloc(2, [512], BF16)
    qfT = alloc(2, [512], BF16)
    qbT = alloc(2, [512], BF16)
    kT = alloc(2, [512], BF16)
    sg = alloc(4, [512], F32)
    cgt = alloc(4, [512], F32)
    zb = alloc(4, [516], F32)
    cstb = AB(SB, zb.off, 1, [NCST], F32)
    wsTf = AB(SB, zb.off + 4096, 1, [512], F32)
    yT = alloc(8, [512], BF16)
    scm = alloc(2, [512], BF16)
    o_sb = alloc(2, [512], F32)
    SfB = alloc(4, [512], BF16)
    SbL = alloc(4, [512], BF16)
    sbst = SbL
    hcg = alloc(1, [8], F32)
    endA = cur[0]
    cur[0] = persist_end
    hidT = alloc(22, [512], BF16)
    sgt = alloc(2, [512], F32)
    xTn = alloc(8, [512], F32)
    rb2 = alloc(1, [512], F32)
    endB = cur[0]
    cur[0] = persist_end
    uT = alloc(8, [512], F32)
    vg = alloc(4, [1024], F32)
    vsq = alloc(1, [1024], F32)
    vn = alloc(4, [1024], BF16)
    mT = alloc(8, [512], BF16)
    stt_ = alloc(2, [512], F32)
    rv = alloc(1, [8], F32)
    endC = cur[0]
    build.mem = (persist_end, endA, endB, endC, TOTAL)
    assert max(endA, endB, endC) <= TOTAL, (persist_end, endA, endB, endC, TOTAL)

    NPS = 5
    pst = [nc.alloc_psum_tensor("ps%d" % i, [128, 512], F32) for i in range(NPS + 1)]
    ptb_q = nc.alloc_psum_tensor("ptq", [128, 1024], BF16)
    ptb_k = nc.alloc_psum_tensor("ptk", [128, 1024], BF16)

    class _PT:
        def __getitem__(self, k):
            p, i, sl = k
            if sl == slice(None):
                sl = slice(0, 512)
            return (ptb_q if i == 0 else ptb_k)[p, sl]
    ptb_t = _PT()
    prot = [0]

    class PB:
        def __init__(self, i):
            self.t = pst[i]
            self.key = ("ps%d" % i, 0, 1)

        def __getitem__(self, k):
            return self.t[k]

        def r(self):
            return self.key

    def pnext():
        i = prot[0] % NPS
        prot[0] += 1
        return PB(i)

    SSB = PB(NPS)

    class _SSB2:
        def __getitem__(self, k):
            return ptb_q[:, :].bitcast(F32)[k]

        def r(self):
            return ("ptq", 0, 1)
    SSB2 = _SSB2()

    def ptr(i):
        return ("ptq" if i == 0 else "ptk", 0, 1)

    def mm(out, lhsT, rhs, start, stop, reads, writes):
        S.add("pe", lambda e: e.matmul(out, lhsT, rhs, start=start, stop=stop), reads, writes)

    def tr(out, in_, ident, reads, writes):
        S.add("pe", lambda e: e.transpose(out, in_, ident), reads, writes)

    def act(out, in_, func, reads, writes, bias=None, scale=None, accum=None):
        kw = {}
        if bias is not None:
            kw["bias"] = bias
        if scale is not None:
            kw["scale"] = scale
        if accum is not None:
            kw["accum_out"] = accum
        S.add("act", lambda e: e.activation(out=out, in_=in_, func=func, **kw), reads, writes)

    def tt(out, in0, in1, op, reads, writes, eng="dve"):
        S.add(eng, lambda e: e.tensor_tensor(out=out, in0=in0, in1=in1, op=op), reads, writes)

    def ts(out, in0, s1, s2, op0, op1, reads, writes, eng="dve"):
        if s2 is None:
            S.add(eng, lambda e: e.tensor_scalar(out=out, in0=in0, scalar1=s1, scalar2=None, op0=op0), reads, writes)
        else:
            S.add(eng, lambda e: e.tensor_scalar(out=out, in0=in0, scalar1=s1, scalar2=s2, op0=op0, op1=op1), reads, writes)

    def stt(out, in0, scalar, in1, op0, op1, reads, writes, eng="dve"):
        S.add(eng, lambda e: e.scalar_tensor_tensor(out=out, in0=in0, scalar=scalar, in1=in1, op0=op0, op1=op1), reads, writes)

    def cp(out, in_, reads, writes, eng="dve"):
        S.add(eng, lambda e: e.tensor_copy(out=out, in_=in_), reads, writes)

    def recip(out, in_, reads, writes):
        S.add("dve", lambda e: e.reciprocal(out=out, in_=in_), reads, writes)

    def memset(out, val, writes, eng="dve"):
        S.add(eng, lambda e: e.memset(out, val), (), writes)

    def dma(q, out, in_, reads, writes, chan):
        S.add(q, lambda e: e.dma_start(out=out, in_=in_), reads, writes, chan=chan)

    dma("sp", cstb[:, 0, :], cst_d[:, :], (), [cstb.r()], "cst")
    dma("sp", identF[:, 0, :], cst_d[:, C_ID:C_ID + 128], (), [identF.r()], "cst")
    dma("sp", vec[:, 0, :], vecT_d[:, :], (), [vec.r()], "cst")
    dma("sp", rows[:, 0, :], rowv_d[:].partition_broadcast(128), (), [rows.r()], "cst")
    dma("sp", wsTf[:, 0, :], wsT_d[:, :], (), [wsTf.r()], "cst")
    for d_ in range(2):
        dma("sp", sg[:, 2 + (1 - d_), :].rearrange("p (h v) -> p h v", h=4), st0[d_].rearrange("h k v -> k h v"), (), [sg.r(2 + (1 - d_))], "sti%d" % d_)
    identf = identF[:, 0, :]
    cp(idb[:, 0, :], identf, [identF.r()], [idb.r()])
    memset(onesb[:, 0, :], 1.0, [onesb.r()])
    cp(wsTb[:, 0, :], wsTf[:, 0, :], [wsTf.r()], [wsTb.r()])
    act(ecol[:, 0, :], rows[:, 0, R_DL:R_DL + 8], AF.Exp, [rows.r()], [ecol.r()], scale=-1.0)
    act(nlg[:, 0, :], ecol[:, 0, :], AF.Ln, [ecol.r()], [nlg.r()], bias=1.0)
    for h in range(4):
        nf = nlg[:, 0, h:h + 1]
        nb = nlg[:, 0, 4 + h:5 + h]
        act(DT[:, h, :], cstb[:, 0, C_NA:C_NA + 128], AF.Exp, [cstb.r(), nlg.r()], [DT.r(h)], scale=nf, bias=QSCALE_LN)
        tt(DT[:, h, :], DT[:, h, :], cstb[:, 0, C_M1:C_M1 + 128], ALU.mult, [DT.r(h), cstb.r()], [DT.r(h)])
        act(dtmp[:, 0, :], cstb[:, 0, C_NB:C_NB + 128], AF.Exp, [cstb.r(), nlg.r()], [dtmp.r()], scale=nb, bias=QSCALE_LN)
        tt(dtmp[:, 0, :], dtmp[:, 0, :], cstb[:, 0, C_M2:C_M2 + 128], ALU.mult, [dtmp.r(), cstb.r()], [dtmp.r()])
        tt(DT[:, h, :], DT[:, h, :], dtmp[:, 0, :], ALU.add, [DT.r(h), dtmp.r()], [DT.r(h)])
        hs = slice(h * 128, (h + 1) * 128)
        act(crossf[:, 0, hs], cstb[:, 0, C_NI1:C_NI1 + 128], AF.Exp, [cstb.r(), nlg.r()], [crossf.r()], scale=nf, bias=QSCALE_LN)
        act(crossb[:, 0, hs], cstb[:, 0, C_NI2:C_NI2 + 128], AF.Exp, [cstb.r(), nlg.r()], [crossb.r()], scale=nb, bias=QSCALE_LN)
        act(kcol[:, 0, h:h + 1], cstb[:, 0, C_PC:C_PC + 1], AF.Exp, [cstb.r(), nlg.r()], [kcol.r()], scale=nf)
        act(kcol[:, 0, 4 + h:5 + h], cstb[:, 0, C_PC + 1:C_PC + 2], AF.Exp, [cstb.r(), nlg.r()], [kcol.r()], scale=nb)
        act(kcol[:, 0, 8 + h:9 + h], nf, AF.Exp, [nlg.r()], [kcol.r()], scale=-128.0)
        act(kcol[:, 0, 12 + h:13 + h], nb, AF.Exp, [nlg.r()], [kcol.r()], scale=-128.0)
    for h in range(4):
        hs = slice(h * 128, (h + 1) * 128)
        for tab, c in ((kdecf, h), (kdecb, 4 + h), (cdecf, 8 + h), (cdecb, 12 + h)):
            cp(tab[:, 0, hs], kcol[:, 0, c:c + 1].to_broadcast([128, 128]), [kcol.r()], [tab.r()])

    if stage == 1:
        return nc, S.emit(nc)
    castn = [0]

    def cast_units(u0, u1, gate=False):
        castn[0] += 1
        dma("pool", wbf.t[u0:u1].rearrange("u p f -> p u f"), wall[u0:u1].rearrange("u p f -> p u f"),
            [("gate", 0, 1)] if gate else (), [wbf.r(u0, u1)], "scr%d" % castn[0])

    for u in (1, 2, 4, 5):
        cast_units(u, u + 1)

    act(scT[:, 0, :, 0], vec[:, 0, V_C:V_C + 8], AF.Silu, [vec.r()], [scT.r()])
    act(scT[:, 0, :, 1], vec[:, 0, V_CCTX:V_CCTX + 8], AF.Silu, [vec.r()], [scT.r()])
    psA = pnext()
    ci = 0
    for l in range(2):
        for cc in range(12):
            slot = ci % R
            ci += 1
            dma("pool", ring[:, slot, :].rearrange("p (k n) -> p k n", k=8),
                adaw[l, :, cc * 512:(cc + 1) * 512].rearrange("(k p) n -> p k n", p=128), (), [ring.r(slot)], "ada%d" % slot)
            for j in range(4):
                mi = l * 48 + cc * 4 + j
                for kc in range(8):
                    mm(psA[:, 2 * mi:2 * mi + 2], ring[:, slot, kc * 512 + j * 128:kc * 512 + (j + 1) * 128], scT[:, 0, kc, :],
                       kc == 0, kc == 7, [ring.r(slot), scT.r()], [psA.r()])
    tt(modT[:, 0, :].rearrange("p (a b) -> p a b", b=2), psA[:, 0:192].rearrange("p (a b) -> p a b", b=2),
       vec[:, 0, V_ADAB:V_ADAB + 96].unsqueeze(2).to_broadcast([128, 96, 2]), ALU.add, [psA.r(), vec.r()], [modT.r()])

    def mcol(l, part, fc, cond):
        i = ((l * 6 + part) * 8 + fc) * 2 + cond
        return modT[:, 0, i:i + 1]

    modT5 = modT[:, 0, :].rearrange("p (l q f c) -> p l q f c", l=2, q=6, f=8)
    for l in range(2):
        for w in range(2):
            for cond in range(2):
                gi = ((l * 2 + w) * 2 + cond) * 8
                gv = V_NMG if w == 0 else V_NFG
                ts(Gt[:, 0, gi:gi + 8], modT5[:, l, 1 + 3 * w, :, cond], 1.0, None, ALU.add, None, [modT.r()], [Gt.r()])
                tt(Gt[:, 0, gi:gi + 8], Gt[:, 0, gi:gi + 8], vec[:, 0, gv + l * 8:gv + l * 8 + 8], ALU.mult, [Gt.r(), vec.r()], [Gt.r()])

    def gcol(l, w, cond, fc):
        i = ((l * 2 + w) * 2 + cond) * 8 + fc
        return Gt[:, 0, i:i + 1]

    def cast_rest():
        rest = [u for u in range(NUNIT) if u not in (1, 2, 4, 5)]
        for i in range(0, len(rest), 4):
            grp = rest[i:i + 4]
            if grp[-1] - grp[0] == len(grp) - 1:
                cast_units(grp[0], grp[-1] + 1, gate=True)
            else:
                for u in grp:
                    cast_units(u, u + 1, gate=True)

    if stage == 2:
        return nc, S.emit(nc)
    def tile_info(t):
        if t < NS:
            return dict(cond=0, rope=True, xd=xs, yd=ys, row0=t * 512, sample=True, lt=t)
        return dict(cond=1, rope=False, xd=xp, yd=yp, row0=(t - NS) * 512, sample=False, lt=t - NS)

    def chunk_info(t, n):
        if t < NS:
            return dict(first=(t == 0 and n == 0), last=(t == NS - 1 and n == 3), seq=None)
        return dict(first=(n % 2 == 0), last=(n % 2 == 1), seq=2 * (t - NS) + n // 2)

    xcnt = [0]
    ocnt = [0]

    xpref = {}

    def x_dma(t, n):
        ti = tile_info(t)
        s = xcnt[0] % 4
        xcnt[0] += 1
        dma("sp", xin[:, s, :], ti["xd"][ti["row0"] + n * 128:ti["row0"] + (n + 1) * 128, :], (), [xin.r(s)], "xin%d" % s)
        return s

    def prefetch_x(t):
        if 0 <= t < NT and t not in xpref:
            xpref[t] = [x_dma(t, n) for n in range(4)]

    def load_xT(t, only=None, pre=None, squares=True):
        S.tag = 'loadx'
        if pre is None:
            pre = xpref.pop(t, None)
        for n in (range(4) if only is None else only):
            if pre is not None:
                s = pre[n]
            else:
                s = x_dma(t, n)
            for half in range(2):
                pb = pnext()
                for j in range(4):
                    fc = half * 4 + j
                    tr(pb[:, j * 128:(j + 1) * 128], xin[:, s, fc * 128:(fc + 1) * 128], identf, [xin.r(s), identF.r()], [pb.r()])
                S.add("act", lambda e, pb=pb, half=half, n=n: e.activation(
                    out=xT[:, half * 4:half * 4 + 4, n * 128:(n + 1) * 128], in_=pb[:, :].rearrange("p (a b) -> p a b", a=4), func=AF.Copy),
                    [pb.r()], [xT.r(half * 4, half * 4 + 4)])
        if squares:
            for fc in range(8):
                sq_acc(fc)
                sq_step()

    ncnt = [0]

    sqc = [0]

    sq_pend = []

    def sq_acc(fc):
        sl = sqc[0] % 4
        sqc[0] += 1
        act(sqb[:, sl, :], xT[:, fc, :], AF.Square, [xT.r(fc)], [sqb.r(sl)])
        assert len(sq_pend) < 4
        sq_pend.append([2, lambda: mm(SSB[:, :], onesb[:, 0, :], sqb[:, sl, :], fc == 0, fc == 7, [onesb.r(), sqb.r(sl)], [SSB.r()])])

    def sq_step(flush=False):
        while sq_pend and (flush or sq_pend[0][0] <= 0):
            sq_pend.pop(0)[1]()
        for p in sq_pend:
            p[0] -= 1

    def rsqrt_from(dst, dst_r, src, src_r, scale):
        act(dst, src, AF.Ln, [src_r], [dst_r], bias=EPS, scale=scale)
        act(dst, dst, AF.Exp, [dst_r], [dst_r], scale=-0.5)

    def norm_mod(l, w, cond, gain=None):
        sq_step(flush=True)
        S.tag = 'norm'
        rs = ncnt[0] % 2
        ncnt[0] += 1
        rsqrt_from(rbuf[:, rs, :], rbuf.r(rs), SSB[:, :], SSB.r(), 1.0 / 1024.0)
        return rs

    def apply_mod(rs, l, w, cond):
        for fc in range(8):
            s = (fc // 2) % 2
            if fc % 2 == 0:
                tt(tmpf[:, s, :], xT[:, fc, :], rbuf[:, rs, :], ALU.mult, [xT.r(fc), rbuf.r(rs)], [tmpf.r(s)])
                act(hT[:, fc, :], tmpf[:, s, :], AF.Identity, [tmpf.r(s), Gt.r(), modT.r()], [hT.r(fc)],
                    bias=mcol(l, 3 * w, fc, cond), scale=gcol(l, w, cond, fc))
            else:
                stt(tmpf[:, 2 + s, :], xT[:, fc, :], gcol(l, w, cond, fc), rbuf[:, rs, :], ALU.mult, ALU.mult,
                    [xT.r(fc), Gt.r(), rbuf.r(rs)], [tmpf.r(2 + s)])
                ts(hT[:, fc, :], tmpf[:, 2 + s, :], mcol(l, 3 * w, fc, cond), None, ALU.add, None, [tmpf.r(2 + s), modT.r()], [hT.r(fc)])

    def rope_evac(pb, dst, slot_ap_r, n_local, post_tab=None, dst_r=None):
        rsl, rr = slot_ap_r
        cosb = ropeb[:, rsl, 0 + n_local, :]
        sinb = ropeb[:, rsl, 4 + n_local, :]
        nsinb = ropeb[:, rsl, 8 + n_local, :]
        s = 0
        pb4 = pb[:, :].rearrange("p (h a d) -> p h a d", h=4, a=2)
        ta4 = qkf[:, s, :].rearrange("p (h a d) -> p h a d", h=4, a=2)
        tb4 = rtb[:, s, :].rearrange("p (h a d) -> p h a d", h=4, a=2)
        tt(ta4, pb4, cosb.unsqueeze(1).unsqueeze(1).to_broadcast([128, 4, 2, 64]), ALU.mult, [pb.r(), rr], [qkf.r(s)])
        tt(tb4[:, :, 0, :], pb4[:, :, 1, :], nsinb.unsqueeze(1).to_broadcast([128, 4, 64]), ALU.mult, [pb.r(), rr], [rtb.r(s)])
        tt(tb4[:, :, 1, :], pb4[:, :, 0, :], sinb.unsqueeze(1).to_broadcast([128, 4, 64]), ALU.mult, [pb.r(), rr], [rtb.r(s)])
        if post_tab is None:
            tt(dst, qkf[:, s, :], rtb[:, s, :], ALU.add, [qkf.r(s), rtb.r(s)], [dst_r])
        else:
            tt(qkf[:, s, :], qkf[:, s, :], rtb[:, s, :], ALU.add, [qkf.r(s), rtb.r(s)], [qkf.r(s)])
        return s

    rope_cnt = [0]
    ropeload = [0]

    def load_rope(t):
        rsl = ropeload[0] % 2
        ropeload[0] += 1
        dma("sp", ropeb[:, rsl, :, :], rope_d[t].rearrange("p a n d -> p (a n) d"), (), [ropeb.r(rsl)], "rope%d" % rsl)
        return rsl

    def init_state(buf, d, ci):
        if ci["seq"] is None:
            cp(buf[:, 0, :], sg[:, 2 + (1 - d), :], [sg.r(2 + (1 - d))], [buf.r()])
        else:
            memset(buf[:, 0, :], 0.0, [buf.r()])

    for slot, u in enumerate((1, 2, 4, 5)):
        dma("sp", ring[:, slot, :], wbf.t[u], [wbf.r(u)], [ring.r(slot)], "w%d" % slot)
    gate_t = max(NT - 4, 0)
    for t in range(NT - 1, -1, -1):
        ti = tile_info(t)
        cond = ti["cond"]
        if ti["rope"]:
            rsl = load_rope(ti["lt"])
        if t == NT - 1:
            load_xT(t)
            prefetch_x(t - 1)
        else:
            for fc in range(8):
                sq_acc(fc)
                sq_step()
        nxt_pre = xpref.pop(t - 1, None) if t - 1 >= 0 else None
        if t == gate_t:
            memset(hcg[:, 0, 0:1], 0.0, [("gate", 0, 1), hcg.r()])
            cast_rest()
        rs = norm_mod(0, 0, cond)
        apply_mod(rs, 0, 0, cond)
        ph = pnext()
        for m in range(8):
            slot = 2 + m // 4
            mc = m % 4
            for kc in range(8):
                mm(ph[:, 2 * m:2 * m + 2], ring[:, slot, (mc * 8 + kc) * 128:(mc * 8 + kc + 1) * 128], hT[:, kc, 0:512:511],
                   kc == 0, kc == 7, [ring.r(slot), hT.r(kc)], [ph.r()])
        cp(hcg[:, 0, :], ph[:, 0:8], [ph.r()], [hcg.r()])
        tt(zh[:, 0, t, :], ph[:, 8:16], hcg[:, 0, :], ALU.mult, [ph.r(), hcg.r()], [zh.r()])
        for n in range(3, -1, -1):
            if t - 1 >= 0:
                S.tag = 'loadx'
                load_xT(t - 1, only=[3 - n], pre=nxt_pre, squares=False)
                if n == 0:
                    prefetch_x(t - 2)
            ci = chunk_info(t, n)
            pk = pnext()
            pv = pnext()
            for kc in range(8):
                mm(pk[:, :], hT[:, kc, n * 128:(n + 1) * 128], ring[:, 0, kc * 512:(kc + 1) * 512], kc == 0, kc == 7, [hT.r(kc), ring.r(0)], [pk.r()])
            for kc in range(8):
                mm(pv[:, :], hT[:, kc, n * 128:(n + 1) * 128], ring[:, 1, kc * 512:(kc + 1) * 512], kc == 0, kc == 7, [hT.r(kc), ring.r(1)], [pv.r()])
            if ti["rope"]:
                s = rope_evac(pk, None, (rsl, ropeb.r(rsl)), n, post_tab=True)
                act(k_tm[:, n, :], qkf[:, s, :], AF.Copy, [qkf.r(s)], [k_tm.r(n)])
                tt(kdf[:, n, :], qkf[:, s, :], kdecb[:, 0, :], ALU.mult, [qkf.r(s), kdecb.r()], [kdf.r(n)])
            else:
                act(k_tm[:, n, :], pk[:, :], AF.Copy, [pk.r()], [k_tm.r(n)])
                tt(kdf[:, n, :], pk[:, :], kdecb[:, 0, :], ALU.mult, [pk.r(), kdecb.r()], [kdf.r(n)])
            act(v_tm[:, n, :], pv[:, :], AF.Copy, [pv.r()], [v_tm.r(n)])
            pU = pnext()
            for h in range(4):
                hs = slice(h * 128, (h + 1) * 128)
                mm(pU[:, hs], kdf[:, n, hs], v_tm[:, n, hs], True, True, [kdf.r(n), v_tm.r(n)], [pU.r()])
            if ci["last"]:
                scur[0] ^= 1
                init_state(Srun[scur[0]], 1, ci)
            Sb_run = Srun[scur[0]]
            cp(sbst[:, n, :], Sb_run[:, 0, :], [Sb_run.r()], [sbst.r(n)])
            tt(Sb_run[:, 0, :], Sb_run[:, 0, :], cdecb[:, 0, :], ALU.mult, [Sb_run.r(), cdecb.r()], [Sb_run.r()])
            tt(Sb_run[:, 0, :], Sb_run[:, 0, :], pU[:, :], ALU.add, [Sb_run.r(), pU.r()], [Sb_run.r()])
            if ci["first"] and ci["seq"] is not None:
                dma("sp", nst[ci["seq"], 1].rearrange("h k v -> k h v"), Sb_run[:, 0, :].rearrange("p (h v) -> p h v", h=4),
                    [Sb_run.r()], (), "nst")
        dma("sp", sbscr.t[t], sbst[:, :, :], [sbst.r()], [sbscr.r(t)], "sbst")
        dma("sp", kvscr.t[t, :, 0:4, :], k_tm[:, :, :], [k_tm.r()], [kvscr.r(t)], "kvst")
        dma("sp", kvscr.t[t, :, 4:8, :], v_tm[:, :, :], [v_tm.r()], [kvscr.r(t)], "kvst")

    if stage == 3:
        return nc, S.emit(nc, final_chans=['nst', 'sbst'])
    wpos = [0]
    wloaded = [0]
    total_units = NT * NUNIT

    def w_ensure():
        while wloaded[0] < min(wpos[0] // 32 + R, total_units):
            g = wloaded[0]
            slot = g % R
            dma("sp", ring[:, slot, :], wbf.t[g % NUNIT], [wbf.r(g % NUNIT)], [ring.r(slot)], "w%d" % slot)
            wloaded[0] += 1

    def wpeek(off, n):
        p = wpos[0] + off
        u = p // 32
        b = p % 32
        assert b + n <= 32 and u < wpos[0] // 32 + R
        w_ensure()
        slot = u % R
        return ring[:, slot, b * 128:(b + n) * 128], ring.r(slot)

    def wadv(n):
        wpos[0] += n
        w_ensure()

    WST = weight_stream()

    def wcheck(mat, l, kc, cc, off=0):
        assert WST[(wpos[0] + off) % 1536] == (mat, l, kc, cc), (WST[(wpos[0] + off) % 1536], (mat, l, kc, cc))

    def proj_fm(mat, l, mc_list, n_k, rhs_fn, evac):
        for mc in mc_list:
            pb = pnext()
            for kc in range(n_k):
                wcheck(mat, l, kc, mc)
                wap, wr = wpeek(0, 1)
                rhs, rr = rhs_fn(kc)
                mm(pb[:, :], wap, rhs, kc == 0, kc == n_k - 1, [wr, rr], [pb.r()])
                wadv(1)
            sq_step()
            evac(mc, pb)

    def prework_pieces(tn):
        pre = xpref.pop(tn)
        pieces = []
        pend = []

        def step_pend(flush=False):
            while pend and (flush or pend[0][0] <= 0):
                pend.pop(0)[1]()
            for p in pend:
                p[0] -= 1

        for n in range(4):
            for half in range(2):
                def p(n=n, half=half):
                    pb = pnext()
                    for j in range(4):
                        fc = half * 4 + j
                        tr(pb[:, j * 128:(j + 1) * 128], xin[:, pre[n], fc * 128:(fc + 1) * 128], identf, [xin.r(pre[n]), identF.r()], [pb.r()])
                    S.add("act", lambda e: e.activation(
                        out=xTn[:, half * 4:half * 4 + 4, n * 128:(n + 1) * 128], in_=pb[:, :].rearrange("p (a b) -> p a b", a=4), func=AF.Copy),
                        [pb.r()], [xTn.r(half * 4, half * 4 + 4)])
                    if n == 3 and half == 1:
                        prefetch_x(tn + 1)
                pieces.append(p)
        for fc in range(8):
            def q(fc=fc):
                sl = sqc[0] % 4
                sqc[0] += 1
                act(sqb[:, sl, :], xTn[:, fc, :], AF.Square, [xTn.r(fc)], [sqb.r(sl)])
                pend.append([2, lambda: mm(SSB[:, :], onesb[:, 0, :], sqb[:, sl, :], fc == 0, fc == 7, [onesb.r(), sqb.r(sl)], [SSB.r()])])
                step_pend()
            pieces.append(q)
        pieces.append(lambda: step_pend())
        pieces.append(lambda: step_pend(flush=True))
        return pieces

    def ffn(l, cond, pre_tile=None):
        rs = norm_mod(l, 1, cond)
        apply_mod(rs, l, 1, cond)
        S.tag = 'ffn_gu'
        pieces = prework_pieces(pre_tile) if pre_tile is not None else []
        for m in range(22):
            pg = pnext()
            pu = pnext()
            for kc in range(8):
                for pb, mat in ((pg, "w_gate"), (pu, "w_up")):
                    wcheck(mat, l, kc, m)
                    wap, wr = wpeek(0, 1)
                    mm(pb[:, :], wap, hT[:, kc, :], kc == 0, kc == 7, [wr, hT.r(kc)], [pb.r()])
                    wadv(1)
            s = m % 2
            act(sgt[:, s, :], pg[:, :], AF.Silu, [pg.r()], [sgt.r(s)])
            tt(hidT[:, m, :], sgt[:, s, :], pu[:, :], ALU.mult, [sgt.r(s), pu.r()], [hidT.r(m)])
            if pieces and m >= 1:
                pieces.pop(0)()
        while pieces:
            pieces.pop(0)()
        if pre_tile is not None:
            cn = tile_info(pre_tile)["cond"]
            rsqrt_from(rb2[:, 0, :], rb2.r(), SSB[:, :], SSB.r(), 1.0 / 1024.0)

        def ev(mc, pb):
            stt(xT[:, mc, :], pb[:, :], mcol(l, 5, mc, cond), xT[:, mc, :], ALU.mult, ALU.add, [pb.r(), modT.r(), xT.r(mc)], [xT.r(mc)])
            sq_acc(mc)
            if pre_tile is not None:
                fc = mc
                s_ = (fc // 2) % 2
                if fc % 2 == 0:
                    tt(tmpf[:, s_, :], xTn[:, fc, :], rb2[:, 0, :], ALU.mult, [xTn.r(fc), rb2.r()], [tmpf.r(s_)])
                    act(hT[:, fc, :], tmpf[:, s_, :], AF.Identity, [tmpf.r(s_), Gt.r(), modT.r()], [hT.r(fc)],
                        bias=mcol(0, 0, fc, cn), scale=gcol(0, 0, cn, fc))
                else:
                    stt(tmpf[:, 2 + s_, :], xTn[:, fc, :], gcol(0, 0, cn, fc), rb2[:, 0, :], ALU.mult, ALU.mult,
                        [xTn.r(fc), Gt.r(), rb2.r()], [tmpf.r(2 + s_)])
                    ts(hT[:, fc, :], tmpf[:, 2 + s_, :], mcol(0, 0, fc, cn), None, ALU.add, None, [tmpf.r(2 + s_), modT.r()], [hT.r(fc)])
        S.tag = 'ffn_dn'
        proj_fm("w_down", l, range(8), 22, lambda kc: (hidT[:, kc, :], hidT.r(kc)), ev)

    preworked = set()

    def chk(k):
        if stage == k:
            raise StopBuild()

    try:
      for t in range(NT):
        ti = tile_info(t)
        cond = ti["cond"]
        if ti["rope"]:
            rsl = load_rope(ti["lt"])
        dma("sp", SbL[:, :, :], sbscr.t[t], [sbscr.r(t)], [SbL.r()], "sbl")
        dma("sp", k_tm[:, :, :], kvscr.t[t, :, 0:4, :], [kvscr.r(t)], [k_tm.r()], "kvl")
        dma("sp", v_tm[:, :, :], kvscr.t[t, :, 4:8, :], [kvscr.r(t)], [v_tm.r()], "kvl")
        if t in preworked:
            S.tag = 'loadx'
            for fc in range(8):
                if fc % 2 == 0:
                    act(xT[:, fc, :], xTn[:, fc, :], AF.Copy, [xTn.r(fc)], [xT.r(fc)])
                else:
                    cp(xT[:, fc, :], xTn[:, fc, :], [xTn.r(fc)], [xT.r(fc)])
        else:
            load_xT(t)
            prefetch_x(t + 1)
            rs = norm_mod(0, 0, cond)
            apply_mod(rs, 0, 0, cond)
        S.tag = 'qkv'
        def qkv_evac(grp, n, pb):
            if grp == 0:
                if ti["rope"]:
                    rope_evac(pb, q_tm[:, n, :], (rsl, ropeb.r(rsl)), n, dst_r=q_tm.r(n))
                else:
                    act(q_tm[:, n, :], pb[:, :], AF.Copy, [pb.r()], [q_tm.r(n)])
            elif grp == 1:
                if ti["rope"]:
                    s_ = rope_evac(pb, None, (rsl, ropeb.r(rsl)), n, post_tab=True)
                    act(k_tm[:, n, :], qkf[:, s_, :], AF.Copy, [qkf.r(s_)], [k_tm.r(n)])
                    tt(kdf[:, n, :], qkf[:, s_, :], kdecf[:, 0, :], ALU.mult, [qkf.r(s_), kdecf.r()], [kdf.r(n)])
                else:
                    act(k_tm[:, n, :], pb[:, :], AF.Copy, [pb.r()], [k_tm.r(n)])
                    tt(kdf[:, n, :], pb[:, :], kdecf[:, 0, :], ALU.mult, [pb.r(), kdecf.r()], [kdf.r(n)])
            else:
                act(v_tm[:, n, :], pb[:, :], AF.Copy, [pb.r()], [v_tm.r(n)])

        for grp in range(3):
            if grp == 0:
                pbs = [pnext() for _ in range(4)]
                for kc in range(8):
                    wcheck("w_in_ab", 0, kc, grp * 4, off=kc * 4)
                    wap, wr = wpeek(kc * 4, 4)
                    for n in range(4):
                        mm(pbs[n][:, :], hT[:, kc, n * 128:(n + 1) * 128], wap, kc == 0, kc == 7, [hT.r(kc), wr], [pbs[n].r()])
                for n in range(4):
                    qkv_evac(grp, n, pbs[n])
            elif grp == 1:
                for n in range(4):
                    tt(kdf[:, n, :], k_tm[:, n, :], kdecf[:, 0, :], ALU.mult, [k_tm.r(n), kdecf.r()], [kdf.r(n)])
            wadv(32)
        chk(41)
        S.tag = 'gconv'
        hrhs = lambda kc: (hT[:, kc, :], hT.r(kc))

        if ti["sample"]:
            if t > 0:
                cp(zb[:, :, 0:1], zh[:, 0, t - 1, 1:8:2].unsqueeze(2), [zh.r()], [zb.r()])
            else:
                memset(zb[:, :, 0:1], 0.0, [zb.r()])
            if t < NS - 1:
                cp(zb[:, :, 513:514], zh[:, 0, t + 1, 0:8:2].unsqueeze(2), [zh.r()], [zb.r()])
            else:
                memset(zb[:, :, 513:514], 0.0, [zb.r()])
            segs = None
        else:
            segs = [(0, 256), (256, 512)]

        def conv_mc(mc):
            w0 = vec[:, 0, V_CONVW + mc:V_CONVW + mc + 1]
            w1 = vec[:, 0, V_CONVW + 4 + mc:V_CONVW + 5 + mc]
            w2 = vec[:, 0, V_CONVW + 8 + mc:V_CONVW + 9 + mc]
            cb = vec[:, 0, V_CONVB + mc:V_CONVB + mc + 1]
            act(cgt[:, mc, :], zb[:, mc, 1:513], AF.Identity, [zb.r(mc), vec.r()], [cgt.r(mc)], bias=cb, scale=w1)
            if segs is None:
                stt(cgt[:, mc, :], zb[:, mc, 0:512], w0, cgt[:, mc, :], ALU.mult, ALU.add, [zb.r(mc), vec.r(), cgt.r(mc)], [cgt.r(mc)])
                stt(cgt[:, mc, :], zb[:, mc, 2:514], w2, cgt[:, mc, :], ALU.mult, ALU.add, [zb.r(mc), vec.r(), cgt.r(mc)], [cgt.r(mc)])
            else:
                for (a, b) in segs:
                    stt(cgt[:, mc, a + 1:b], zb[:, mc, a + 1:b], w0, cgt[:, mc, a + 1:b], ALU.mult, ALU.add, [zb.r(mc), vec.r(), cgt.r(mc)], [cgt.r(mc)])
                    stt(cgt[:, mc, a:b - 1], zb[:, mc, a + 2:b + 1], w2, cgt[:, mc, a:b - 1], ALU.mult, ALU.add, [zb.r(mc), vec.r(), cgt.r(mc)], [cgt.r(mc)])

        fillers = []
        for i in range(4):
            fillers.append(lambda i=i: proj_fm("w_in_ab", 0, [12 + i], 8, hrhs,
                           lambda mc, pb: act(sg[:, mc - 12, :], pb[:, :], AF.Silu, [pb.r()], [sg.r(mc - 12)])))
        for i in range(4):
            fillers.append(lambda i=i: proj_fm("w_in_ab", 0, [20 + i], 8, hrhs,
                           lambda mc, pb: act(cgt[:, mc - 20, :], pb[:, :], AF.Copy, [pb.r()], [cgt.r(mc - 20)])))
        for i in range(4):
            def f_xc(i=i):
                proj_fm("w_in_ab", 0, [24 + i], 8, hrhs,
                        lambda mc, pb: tt(zb[:, mc - 24, 1:513], pb[:, :], cgt[:, mc - 24, :], ALU.mult, [pb.r(), cgt.r(mc - 24)], [zb.r(mc - 24)]))
                conv_mc(i)
            fillers.append(f_xc)
        for i in range(4):
            fillers.append(lambda i=i: proj_fm("w_in_ab", 0, [16 + i], 8, hrhs,
                           lambda mc, pb: tt(yT[:, 4 + mc - 16, :], pb[:, :], cgt[:, mc - 16, :], ALU.mult, [pb.r(), cgt.r(mc - 16)], [yT.r(4 + mc - 16)])))
        fi = [0]

        def fill():
            if fi[0] < len(fillers):
                tg = S.tag
                S.tag = 'gconv'
                fillers[fi[0]]()
                fi[0] += 1
                S.tag = tg
        fill()
        fill()
        chk(4)
        S.tag = 'state'
        for n in range(4):
            ci = chunk_info(t, n)
            if ci["first"]:
                scur[0] ^= 1
                init_state(Srun[scur[0]], 0, ci)
            Sf_run = Srun[scur[0]]
            cp(SfB[:, n, :], Sf_run[:, 0, :], [Sf_run.r()], [SfB.r(n)])
            pU = pnext()
            for h in range(4):
                hs = slice(h * 128, (h + 1) * 128)
                mm(pU[:, hs], kdf[:, n, hs], v_tm[:, n, hs], True, True, [kdf.r(n), v_tm.r(n)], [pU.r()])
            tt(Sf_run[:, 0, :], Sf_run[:, 0, :], cdecf[:, 0, :], ALU.mult, [Sf_run.r(), cdecf.r()], [Sf_run.r()])
            tt(Sf_run[:, 0, :], Sf_run[:, 0, :], pU[:, :], ALU.add, [Sf_run.r(), pU.r()], [Sf_run.r()])
            if ci["last"] and ci["seq"] is not None:
                dma("sp", nst[ci["seq"], 0].rearrange("h k v -> k h v"), Sf_run[:, 0, :].rearrange("p (h v) -> p h v", h=4),
                    [Sf_run.r()], (), "nst")
        chk(5)
        S.tag = 'ret'
        for h in range(4):
            hs = slice(h * 128, (h + 1) * 128)
            s = h % 2
            for n in range(4):
                ns = slice(n * 128, (n + 1) * 128)
                tr(ptb_t[:, 0, ns], q_tm[:, n, hs], idb[:, 0, :], [q_tm.r(n), idb.r()], [ptr(0)])
            for n in range(4):
                ns = slice(n * 128, (n + 1) * 128)
                tr(ptb_t[:, 1, ns], k_tm[:, n, hs], idb[:, 0, :], [k_tm.r(n), idb.r()], [ptr(1)])
            act(qT[:, s, :], ptb_t[:, 0, :], AF.Copy, [ptr(0)], [qT.r(s)])
            bc = lambda tab: tab[:, 0, hs].unsqueeze(1).to_broadcast([128, 4, 128])
            v4 = lambda ap: ap.rearrange("p (a b) -> p a b", a=4)
            tt(v4(qfT[:, s, :]), v4(ptb_t[:, 0, :]), bc(crossf), ALU.mult, [ptr(0), crossf.r()], [qfT.r(s)])
            tt(v4(qbT[:, s, :]), v4(ptb_t[:, 0, :]), bc(crossb), ALU.mult, [ptr(0), crossb.r()], [qbT.r(s)])
            act(kT[:, s, :], ptb_t[:, 1, :], AF.Copy, [ptr(1)], [kT.r(s)])
            chk(51)
            fill()
            psc = pnext()
            for n in range(4):
                ns = slice(n * 128, (n + 1) * 128)
                mm(psc[:, ns], kT[:, s, ns], qT[:, s, ns], True, True, [kT.r(s), qT.r(s)], [psc.r()])
            tt(v4(scm[:, s, :]), v4(psc[:, :]), DT[:, h, :].unsqueeze(1).to_broadcast([128, 4, 128]), ALU.mult, [psc.r(), DT.r(h)], [scm.r(s)])
            chk(52)
            fill()
            po = pnext()
            for n in range(4):
                ns = slice(n * 128, (n + 1) * 128)
                mm(po[:, ns], v_tm[:, n, hs], scm[:, s, ns], True, False, [v_tm.r(n), scm.r(s)], [po.r()])
                mm(po[:, ns], SfB[:, n, hs], qfT[:, s, ns], False, False, [SfB.r(n), qfT.r(s)], [po.r()])
                mm(po[:, ns], SbL[:, n, hs], qbT[:, s, ns], False, True, [SbL.r(n), qbT.r(s)], [po.r()])
            chk(53)
            act(o_sb[:, s, :], po[:, :], AF.Copy, [po.r()], [o_sb.r(s)])
            act(sqb[:, s, :], po[:, :], AF.Square, [po.r()], [sqb.r(s)])
            fill()
            pss = pnext()
            mm(pss[:, :], onesb[:, 0, :], sqb[:, s, :], True, True, [onesb.r(), sqb.r(s)], [pss.r()])
            rsqrt_from(rbuf[:, s, :], rbuf.r(s), pss[:, :], pss.r(), 1.0 / 128.0)
            chk(54)
            tt(o_sb[:, s, :], o_sb[:, s, :], rbuf[:, s, :], ALU.mult, [o_sb.r(s), rbuf.r(s)], [o_sb.r(s)])
            stt(yT[:, h, :], o_sb[:, s, :], vec[:, 0, V_RETG + h:V_RETG + h + 1], sg[:, h, :], ALU.mult, ALU.mult,
                [o_sb.r(s), vec.r(), sg.r(h)], [yT.r(h)])
            chk(55 + h)
        while fi[0] < len(fillers):
            fill()
        chk(6)
        S.tag = 'outab'
        def ev_ab(mc, pb):
            stt(xT[:, mc, :], pb[:, :], mcol(0, 2, mc, cond), xT[:, mc, :], ALU.mult, ALU.add, [pb.r(), modT.r(), xT.r(mc)], [xT.r(mc)])
            sq_acc(mc)
        proj_fm("w_out_ab", 0, range(8), 8, lambda kc: (yT[:, kc, :], yT.r(kc)), ev_ab)
        ffn(0, cond)
        chk(7)
        S.tag = 'l1'
        rs = norm_mod(1, 0, cond)
        apply_mod(rs, 1, 0, cond)
        for half in range(2):
            pbs = [pnext() for _ in range(4)]
            for kc in range(8):
                wcheck("w_in_c", 0, kc, 8 + half * 4, off=half * 32 + kc * 4)
                wap, wr = wpeek(half * 32 + kc * 4, 4)
                for n in range(4):
                    mm(pbs[n][:, :], hT[:, kc, n * 128:(n + 1) * 128], wap, kc == 0, kc == 7, [hT.r(kc), wr], [pbs[n].r()])
            for n in range(4):
                act(vg[:, n, half * 512:(half + 1) * 512], pbs[n][:, :], AF.Gelu_apprx_tanh, [pbs[n].r()], [vg.r(n)])
        for n in range(4):
            act(vsq[:, 0, :], vg[:, n, :], AF.Square, [vg.r(n)], [vsq.r(), rv.r()], accum=rv[:, 0, n:n + 1])
        rsqrt_from(rv[:, 0, 4:8], rv.r(), rv[:, 0, 0:4], rv.r(), 1.0 / 1024.0)
        for n in range(4):
            stt(vn[:, n, :], vg[:, n, :], rv[:, 0, 4 + n:5 + n], rows[:, 0, R_CNG:R_CNG + 1024], ALU.mult, ALU.mult,
                [vg.r(n), rv.r(), rows.r()], [vn.r(n)])
        wadv(64)
        proj_fm("w_in_c", 0, range(8), 8, hrhs,
                lambda mc, pb: act(uT[:, mc, :], pb[:, :], AF.Gelu_apprx_tanh, [pb.r()], [uT.r(mc)]))
        for cc in range(8):
            g = cc // 2
            pb = pnext()
            for n in range(4):
                ns = slice(n * 128, (n + 1) * 128)
                mm(pb[:, ns], vn[:, n, cc * 128:(cc + 1) * 128], wsTb[:, 0, g * 128:(g + 1) * 128], True, True, [vn.r(n), wsTb.r()], [pb.r()])
            s = cc % 2
            tt(stt_[:, s, :].rearrange("p (a b) -> p a b", a=4), pb[:, :].rearrange("p (a b) -> p a b", a=4),
               rows[:, 0, R_BS + g * 128:R_BS + (g + 1) * 128].unsqueeze(1).to_broadcast([128, 4, 128]), ALU.add, [pb.r(), rows.r()], [stt_.r(s)])
            tt(mT[:, cc, :], stt_[:, s, :], uT[:, cc, :], ALU.mult, [stt_.r(s), uT.r(cc)], [mT.r(cc)])
        def ev_c(mc, pb):
            stt(xT[:, mc, :], pb[:, :], mcol(1, 2, mc, cond), xT[:, mc, :], ALU.mult, ALU.add, [pb.r(), modT.r(), xT.r(mc)], [xT.r(mc)])
            sq_acc(mc)
        proj_fm("w_out_c", 0, range(8), 8, lambda kc: (mT[:, kc, :], mT.r(kc)), ev_c)
        if t + 1 < NT and stage == 9:
            prefetch_x(t + 1)
            preworked.add(t + 1)
            ffn(1, cond, pre_tile=t + 1)
        else:
            ffn(1, cond)
        chk(8)
        S.tag = 'final'
        rs = norm_mod(0, 0, cond)
        for fc in range(8):
            stt(xT[:, fc, :], xT[:, fc, :], vec[:, 0, V_FING + fc:V_FING + fc + 1], rbuf[:, rs, :], ALU.mult, ALU.mult,
                [xT.r(fc), vec.r(), rbuf.r(rs)], [xT.r(fc)])
        for n in range(4):
            s = ocnt[0] % 2
            ocnt[0] += 1
            for half in range(2):
                pb = pnext()
                for j in range(4):
                    fc = half * 4 + j
                    tr(pb[:, j * 128:(j + 1) * 128], xT[:, fc, n * 128:(n + 1) * 128], identf, [xT.r(fc), identF.r()], [pb.r()])
                act(ost[:, s, half * 512:(half + 1) * 512], pb[:, :], AF.Copy, [pb.r()], [ost.r(s)])
            dma("sp", ti["yd"][ti["row0"] + n * 128:ti["row0"] + (n + 1) * 128, :], ost[:, s, :], [ost.r(s)], (), "yout")

    except StopBuild:
        pass
    stats = S.emit(nc, final_chans=["yout", "nst"])
    build.last_sched = S
    return nc, stats


def _consts():
    c = np.zeros((128, NCST), np.float32)
    j = np.arange(128, dtype=np.float32)[:, None]
    i = np.arange(128, dtype=np.float32)[None, :]
    c[:, C_ID:C_ID + 128] = np.eye(128, dtype=np.float32)
    c[:, C_NA:C_NA + 128] = -np.maximum(i - j, 0.0)
    c[:, C_NB:C_NB + 128] = -np.maximum(j - i, 0.0)
    c[:, C_M1:C_M1 + 128] = (i >= j)
    c[:, C_M2:C_M2 + 128] = (j >= i)
    c[:, C_NI1:C_NI1 + 128] = -(i + 1.0)
    c[:, C_NI2:C_NI2 + 128] = -(128.0 - i)
    c[:, C_PC] = -(127.0 - j[:, 0])
    c[:, C_PC + 1] = -j[:, 0]
    return c


def _rope_tables(ns_tiles):
    T = ns_tiles * 512
    pos = np.arange(T)
    row = (pos // 64).astype(np.float32)
    col = (pos % 64).astype(np.float32)
    freqs = (np.float32(10000.0) ** (-np.arange(32, dtype=np.float32) / np.float32(32))).astype(np.float32)
    ang = np.concatenate([row[:, None] * freqs, col[:, None] * freqs], axis=-1).astype(np.float32)
    cos = np.cos(ang).astype(np.float32)
    sin = np.sin(ang).astype(np.float32)
    tab = np.stack([cos, sin, -sin], axis=0)
    tab = tab.reshape(3, ns_tiles, 4, 128, 64).transpose(1, 3, 0, 2, 4)
    return np.ascontiguousarray(tab)


def _pack_weights(inp):
    st = weight_stream()
    mats = {k: np.asarray(inp[k]) for k in ("w_in_ab", "w_out_ab", "w_gate", "w_up", "w_down", "w_in_c", "w_out_c")}
    wall = np.empty((NUNIT, 128, 32, 128), np.float32)
    for p, (m, l, kc, cc) in enumerate(st):
        wall[p // 32, :, p % 32, :] = mats[m][l, kc * 128:(kc + 1) * 128, cc * 128:(cc + 1) * 128]
    return wall.reshape(NUNIT, 128, 4096)


def _shared_inputs(inp):
    f = lambda k: np.asarray(inp[k], np.float32)
    cols = [f("ada_b").reshape(96, 128).T]
    cols.append(None)
    cols.append(f("c_ctx").reshape(8, 128).T)
    cols.append(f("norm_mix_g").reshape(16, 128).T)
    cols.append(f("norm_ffn_g").reshape(16, 128).T)
    cols.append(f("ret_norm_g").reshape(4, 128).T)
    cols.append(f("conv_w").reshape(12, 128).T)
    cols.append(f("conv_b").reshape(4, 128).T)
    cols.append(f("final_norm_g").reshape(8, 128).T)
    rowv = np.concatenate([f("c_norm_g").reshape(-1), f("b_spatial").reshape(-1), f("ret_decay_logit").reshape(-1)])
    wsT = np.ascontiguousarray(f("w_spatial")[0].transpose(2, 0, 1).reshape(128, 512))
    return cols, rowv, wsT


_CACHE = {}


def kernel(**inp):
    NS, NP = 8, 2
    key = (NS, NP)
    if key not in _CACHE:
        _CACHE[key] = build(NS, NP)[0]
    nc = _CACHE[key]
    cols, rowv, wsT = _shared_inputs(inp)
    wall = _pack_weights(inp)
    cst = _consts()
    rope = _rope_tables(NS)
    adaw = np.ascontiguousarray(np.asarray(inp["ada_w"], np.float32))
    x_prompt = np.asarray(inp["x_prompt"], np.float32)
    x_sample = np.asarray(inp["x_sample"], np.float32)
    state_ret = np.asarray(inp["state_ret"], np.float32)
    c = np.asarray(inp["c"], np.float32)
    in_maps = []
    for i in range(8):
        cc = list(cols)
        cc[1] = c[i].reshape(8, 128).T
        in_maps.append({
            "xs": np.ascontiguousarray(x_sample[i]),
            "xp": np.ascontiguousarray(x_prompt[4 * i:4 * i + 4].reshape(1024, 1024)),
            "st0": np.ascontiguousarray(state_ret[i, 0]),
            "wall": wall, "adaw": adaw,
            "vecT": np.ascontiguousarray(np.concatenate(cc, axis=1)),
            "rowv": rowv, "wsT": wsT, "cst": cst, "ropeT": rope,
        })
    res = run_bass_kernel_spmd(nc, in_maps, core_ids=list(range(8))).results
    y_prompt = np.concatenate([r["yp"].reshape(4, 256, 1024) for r in res], axis=0)
    y_sample = np.stack([r["ys"] for r in res], axis=0)
    nstate = np.concatenate([r["nst"].reshape(4, 1, 2, 4, 128, 128) for r in res], axis=0)
    return (y_prompt.astype(np.float32), y_sample.astype(np.float32), nstate.astype(np.float32))
```

```python
import math
import numpy as np
import concourse.bass as bass
import concourse.mybir as mybir
from concourse.bass_utils import run_bass_kernel_spmd

F32 = mybir.dt.float32
BF16 = mybir.dt.bfloat16
AF = mybir.ActivationFunctionType
ALU = mybir.AluOpType

COMPUTE = ("pe", "act", "dve", "pool")
ENGS = ("pe", "act", "dve", "pool", "sp")
SEM_WRAP = 30000
BLK = 256
EPS = 1e-6
NWARM = 12
QSCALE_LN = math.log(128.0 ** -0.5)


class StopBuild(Exception):
    pass


class Op:
    __slots__ = ("eng", "fn", "idx", "seq", "waits", "signaled", "sigval", "chan", "chanval", "known", "tag")


class Sched:
    def __init__(self):
        self.ops = {e: [] for e in ENGS}
        self.recs = {}
        self.chan_tot = {}
        self.seq = 0
        self.tag = ''

    def add(self, eng, fn, reads=(), writes=(), chan=None):
        o = Op()
        o.eng = eng
        o.fn = fn
        o.chan = chan
        o.signaled = False
        o.sigval = 0
        o.tag = self.tag
        lst = self.ops[eng]
        o.idx = len(lst)
        o.seq = self.seq
        self.seq += 1
        deps = {}
        rkey = chan if chan is not None else eng
        recs = self.recs
        psr = [r for r in reads if r[0][0] == "p"]
        if psr:
            writes = list(writes) + psr
        for (key, i0, i1) in reads:
            for i in range(i0, i1):
                r = recs.get((key, i))
                if r is not None and r[0] is not None:
                    d = r[0]
                    deps[d.seq] = d
        for (key, i0, i1) in writes:
            for i in range(i0, i1):
                r = recs.get((key, i))
                if r is not None:
                    d = r[0]
                    if d is not None and not (d.chan is None and chan is None and d.eng == eng):
                        deps[d.seq] = d
                    for d in r[1].values():
                        if not (d.chan is None and chan is None and d.eng == eng):
                            deps[d.seq] = d
        for (key, i0, i1) in reads:
            for i in range(i0, i1):
                r = recs.get((key, i))
                if r is None:
                    r = [None, {}]
                    recs[(key, i)] = r
                r[1][rkey] = o
        for (key, i0, i1) in writes:
            for i in range(i0, i1):
                recs[(key, i)] = [o, {}]
        known = dict(lst[-1].known) if lst else {}
        waits = []
        for sq in sorted(deps.keys(), reverse=True):
            d = deps[sq]
            if d.chan is None:
                if d.eng == "pe" and eng == "pe" and chan is None:
                    continue
                k, v = d.eng, d.idx
            else:
                k, v = d.chan, d.chanval
            if known.get(k, -1) >= v:
                continue
            if d.chan is not None:
                v = self.chan_tot[d.chan]
            else:
                d.signaled = True
            waits.append((k, v, d))
            for kk, vv in d.known.items():
                if known.get(kk, -1) < vv:
                    known[kk] = vv
            if known.get(k, -1) < v:
                known[k] = v
        o.known = known
        o.waits = waits
        if chan is not None:
            self.chan_tot[chan] = self.chan_tot.get(chan, 0) + 1
            o.chanval = self.chan_tot[chan]
        lst.append(o)
        return o

    def emit(self, nc, final_chans=()):
        nsig = {}
        for e in COMPUTE:
            c = 0
            for o in self.ops[e]:
                if o.signaled:
                    c += 1
                    o.sigval = c
            nsig[e] = c
        sems = {}
        for e in COMPUTE:
            n = max(1, (nsig[e] + SEM_WRAP - 1) // SEM_WRAP)
            sems[e] = [nc.alloc_semaphore("s_%s_%d" % (e, i)) for i in range(n)]
        csems = {c: nc.alloc_semaphore("c_%s" % c) for c in self.chan_tot}
        ops = self.ops
        chan_tot = self.chan_tot

        def run(name, eng):
            for o in ops[name]:
                for (k, v, d) in o.waits:
                    if d.chan is None:
                        sv = d.sigval
                        eng.wait_ge(sems[k][(sv - 1) // SEM_WRAP], (sv - 1) % SEM_WRAP + 1)
                    else:
                        eng.wait_ge(csems[k], 16 * v)
                ins = o.fn(eng)
                if o.chan is not None:
                    ins.then_inc(csems[o.chan], 16)
                elif o.signaled:
                    ins.then_inc(sems[name][(o.sigval - 1) // SEM_WRAP], 1)
            if name == "sp":
                for c in chan_tot:
                    eng.wait_ge(csems[c], 16 * chan_tot[c])

        with nc.Block() as block:
            @block.tensor
            def _(e):
                run("pe", e)

            @block.scalar
            def _(e):
                run("act", e)

            @block.vector
            def _(e):
                run("dve", e)

            @block.gpsimd
            def _(e):
                run("pool", e)

            @block.sync
            def _(e):
                run("sp", e)
        return {e: (len(self.ops[e]), nsig.get(e, 0)) for e in ENGS}


class AB:
    def __init__(self, SB, off, n0, inner, dtype):
        es = 2 if dtype == BF16 else 4
        inner = list(inner)
        self.slice_bytes = es * int(np.prod(inner))
        self.off = off
        self.n0 = n0
        self.size = self.slice_bytes * n0
        assert off % 4 == 0 and self.size % 4 == 0
        ap = SB[:, off // 4:(off + self.size) // 4]
        if dtype == BF16:
            ap = ap.bitcast(BF16)
        if len(inner) == 1:
            ap = ap.rearrange("p (a b) -> p a b", a=n0)
        elif len(inner) == 2:
            ap = ap.rearrange("p (a b c) -> p a b c", a=n0, b=inner[0])
        else:
            raise ValueError
        self.ap = ap

    def __getitem__(self, k):
        return self.ap[k]

    def r(self, i0=None, i1=None):
        if i0 is None:
            i0, i1 = 0, self.n0
        elif i1 is None:
            i1 = i0 + 1
        lo = self.off + i0 * self.slice_bytes
        hi = self.off + i1 * self.slice_bytes
        return ("sb", lo // BLK, (hi + BLK - 1) // BLK)


class DB:
    def __init__(self, t, name, n0):
        self.t = t
        self.name = name
        self.n0 = n0

    def r(self, i0=None, i1=None):
        if i0 is None:
            return (self.name, 0, self.n0)
        if i1 is None:
            i1 = i0 + 1
        return (self.name, i0, i1)


def weight_stream():
    st = []
    for grp in range(3):
        for kc in range(8):
            for j in range(4):
                st.append(("w_in_ab", 0, kc, grp * 4 + j))
    for grp in (3, 5, 6, 4):
        for mc in range(4):
            for kc in range(8):
                st.append(("w_in_ab", 0, kc, grp * 4 + mc))
    for mc in range(8):
        for ac in range(8):
            st.append(("w_out_ab", 0, ac, mc))

    def ffn(l):
        for m in range(22):
            for kc in range(8):
                for mat in ("w_gate", "w_up"):
                    st.append((mat, l, kc, m))
        for mc in range(8):
            for kc in range(22):
                st.append(("w_down", l, kc, mc))
    ffn(0)
    for half in range(2):
        for kc in range(8):
            for j in range(4):
                st.append(("w_in_c", 0, kc, 8 + half * 4 + j))
    for mc in range(8):
        for kc in range(8):
            st.append(("w_in_c", 0, kc, mc))
    for mc in range(8):
        for kc in range(8):
            st.append(("w_out_c", 0, kc, mc))
    ffn(1)
    assert len(st) == 1536
    return st


NUNIT = 48
V_ADAB, V_C, V_CCTX, V_NMG, V_NFG, V_RETG, V_CONVW, V_CONVB, V_FING = 0, 96, 104, 112, 128, 144, 148, 160, 164
NV = 172
R_CNG, R_BS, R_DL = 0, 1024, 1536
NR = 1544
C_ID, C_NA, C_NB, C_M1, C_M2, C_NI1, C_NI2, C_PC = 0, 128, 256, 384, 512, 640, 768, 896
NCST = 898


def build(NS, NP, stage=9):
    NT = NS + NP
    nc = bass.Bass("TRN2", target_bir_lowering=False)
    S = Sched()
    R = 4

    def din(name, shape, dt=F32):
        return nc.dram_tensor(name, list(shape), dt, kind="ExternalInput")

    def dout(name, shape):
        return nc.dram_tensor(name, list(shape), F32, kind="ExternalOutput")

    xs = din("xs", [max(NS, 1) * 512, 1024])
    xp = din("xp", [max(NP, 1) * 512, 1024])
    st0 = din("st0", [2, 4, 128, 128])
    wall = din("wall", [NUNIT, 128, 4096])
    adaw = din("adaw", [2, 1024, 6144])
    vecT_d = din("vecT", [128, NV])
    rowv_d = din("rowv", [NR])
    wsT_d = din("wsT", [128, 512])
    cst_d = din("cst", [128, NCST])
    rope_d = din("ropeT", [max(NS, 1), 128, 3, 4, 64])
    ys = dout("ys", [max(NS, 1) * 512, 1024])
    yp = dout("yp", [max(NP, 1) * 512, 1024])
    nst = dout("nst", [max(2 * NP, 1), 2, 4, 128, 128])
    wbf = DB(nc.dram_tensor("wbf", [NUNIT, 128, 4096], BF16, kind="Internal"), "wbf", NUNIT)
    sbscr = DB(nc.dram_tensor("sbscr", [NT, 128, 4, 512], BF16, kind="Internal"), "sbscr", NT)
    kvscr = DB(nc.dram_tensor("kvscr", [NT, 128, 8, 512], BF16, kind="Internal"), "kvscr", NT)

    TOTAL = 207 * 1024
    SB = nc.alloc_sbuf_tensor("SB", [128, TOTAL // 4], F32)
    cur = [0]

    def alloc(n0, inner, dt):
        b = AB(SB, cur[0], n0, inner, dt)
        cur[0] += (b.size + BLK - 1) // BLK * BLK
        return b

    ring = alloc(R, [4096], BF16)
    xin = alloc(4, [1024], F32)
    ost = alloc(2, [1024], F32)
    xT = alloc(8, [512], F32)
    hT = alloc(8, [512], BF16)
    tmpf = alloc(4, [512], F32)
    sqb = alloc(4, [512], BF16)
    rbuf = alloc(2, [512], F32)
    identF = alloc(1, [128], F32)
    vec = alloc(1, [NV], F32)
    rows = alloc(1, [NR], F32)
    idb = alloc(1, [128], BF16)
    onesb = alloc(1, [128], BF16)
    wsTb = alloc(1, [512], BF16)
    nlg = alloc(1, [8], F32)
    ecol = alloc(1, [8], F32)
    DT = alloc(4, [128], F32)
    dtmp = alloc(1, [128], F32)
    crossf = alloc(1, [512], F32)
    crossb = alloc(1, [512], F32)
    kdecf = alloc(1, [512], F32)
    kdecb = alloc(1, [512], F32)
    cdecf = alloc(1, [512], F32)
    cdecb = alloc(1, [512], F32)
    kcol = alloc(1, [24], F32)
    scT = alloc(1, [8, 2], BF16)
    modT = alloc(1, [192], F32)
    Gt = alloc(1, [64], F32)
    ropeb = alloc(2, [12, 64], F32)
    Srun = [alloc(1, [512], F32), alloc(1, [512], F32)]
    scur = [0]
    zh = alloc(1, [NT, 8], F32)
    persist_end = cur[0]
    qkf = alloc(1, [512], F32)
    rtb = alloc(1, [512], F32)
    q_tm = alloc(4, [512], BF16)
    k_tm = alloc(4, [512], BF16)
    kdf = alloc(4, [512], BF16)
    v_tm = alloc(4, [512], BF16)
    qT = alloc(2, [512], BF16)
    qfT = alloc(2, [512], BF16)
    qbT = alloc(2, [512], BF16)
    kT = alloc(2, [512], BF16)
    sg = alloc(4, [512], F32)
    cgt = alloc(4, [512], F32)
    zb = alloc(4, [516], F32)
    cstb = AB(SB, zb.off, 1, [NCST], F32)
    wsTf = AB(SB, zb.off + 4096, 1, [512], F32)
    yT = alloc(8, [512], BF16)
    scm = alloc(2, [512], BF16)
    o_sb = alloc(2, [512], F32)
    SfB = alloc(4, [512], BF16)
    SbL = alloc(4, [512], BF16)
    sbst = SbL
    hcg = alloc(1, [8], F32)
    endA = cur[0]
    cur[0] = persist_end
    hidT = alloc(22, [512], BF16)
    sgt = alloc(2, [512], F32)
    xTn = alloc(8, [512], F32)
    rb2 = alloc(1, [512], F32)
    endB = cur[0]
    cur[0] = persist_end
    uT = alloc(8, [512], F32)
    vg = alloc(4, [1024], F32)
    vsq = alloc(1, [1024], F32)
    vn = alloc(4, [1024], BF16)
    mT = alloc(8, [512], BF16)
    stt_ = alloc(2, [512], F32)
    rv = alloc(1, [8], F32)
    endC = cur[0]
    build.mem = (persist_end, endA, endB, endC, TOTAL)
    assert max(endA, endB, endC) <= TOTAL, (persist_end, endA, endB, endC, TOTAL)

    NPS = 5
    pst = [nc.alloc_psum_tensor("ps%d" % i, [128, 512], F32) for i in range(NPS + 1)]
    ptb_q = nc.alloc_psum_tensor("ptq", [128, 1024], BF16)
    ptb_k = nc.alloc_psum_tensor("ptk", [128, 1024], BF16)

    class _PT:
        def __getitem__(self, k):
            p, i, sl = k
            if sl == slice(None):
                sl = slice(0, 512)
            return (ptb_q if i == 0 else ptb_k)[p, sl]
    ptb_t = _PT()
    prot = [0]

    class PB:
        def __init__(self, i):
            self.t = pst[i]
            self.key = ("ps%d" % i, 0, 1)

        def __getitem__(self, k):
            return self.t[k]

        def r(self):
            return self.key

    def pnext():
        i = prot[0] % NPS
        prot[0] += 1
        return PB(i)

    SSB = PB(NPS)

    class _SSB2:
        def __getitem__(self, k):
            return ptb_q[:, :].bitcast(F32)[k]

        def r(self):
            return ("ptq", 0, 1)
    SSB2 = _SSB2()

    def ptr(i):
        return ("ptq" if i == 0 else "ptk", 0, 1)

    def mm(out, lhsT, rhs, start, stop, reads, writes):
        S.add("pe", lambda e: e.matmul(out, lhsT, rhs, start=start, stop=stop), reads, writes)

    def tr(out, in_, ident, reads, writes):
        S.add("pe", lambda e: e.transpose(out, in_, ident), reads, writes)

    def act(out, in_, func, reads, writes, bias=None, scale=None, accum=None):
        kw = {}
        if bias is not None:
            kw["bias"] = bias
        if scale is not None:
            kw["scale"] = scale
        if accum is not None:
            kw["accum_out"] = accum
        S.add("act", lambda e: e.activation(out=out, in_=in_, func=func, **kw), reads, writes)

    def tt(out, in0, in1, op, reads, writes, eng="dve"):
        S.add(eng, lambda e: e.tensor_tensor(out=out, in0=in0, in1=in1, op=op), reads, writes)

    def ts(out, in0, s1, s2, op0, op1, reads, writes, eng="dve"):
        if s2 is None:
            S.add(eng, lambda e: e.tensor_scalar(out=out, in0=in0, scalar1=s1, scalar2=None, op0=op0), reads, writes)
        else:
            S.add(eng, lambda e: e.tensor_scalar(out=out, in0=in0, scalar1=s1, scalar2=s2, op0=op0, op1=op1), reads, writes)

    def stt(out, in0, scalar, in1, op0, op1, reads, writes, eng="dve"):
        S.add(eng, lambda e: e.scalar_tensor_tensor(out=out, in0=in0, scalar=scalar, in1=in1, op0=op0, op1=op1), reads, writes)

    def cp(out, in_, reads, writes, eng="dve"):
        S.add(eng, lambda e: e.tensor_copy(out=out, in_=in_), reads, writes)

    def recip(out, in_, reads, writes):
        S.add("dve", lambda e: e.reciprocal(out=out, in_=in_), reads, writes)

    def memset(out, val, writes, eng="dve"):
        S.add(eng, lambda e: e.memset(out, val), (), writes)

    def dma(q, out, in_, reads, writes, chan):
        S.add(q, lambda e: e.dma_start(out=out, in_=in_), reads, writes, chan=chan)

    dma("sp", cstb[:, 0, :], cst_d[:, :], (), [cstb.r()], "cst")
    dma("sp", identF[:, 0, :], cst_d[:, C_ID:C_ID + 128], (), [identF.r()], "cst")
    dma("sp", vec[:, 0, :], vecT_d[:, :], (), [vec.r()], "cst")
    dma("sp", rows[:, 0, :], rowv_d[:].partition_broadcast(128), (), [rows.r()], "cst")
    dma("sp", wsTf[:, 0, :], wsT_d[:, :], (), [wsTf.r()], "cst")
    for d_ in range(2):
        dma("sp", sg[:, 2 + (1 - d_), :].rearrange("p (h v) -> p h v", h=4), st0[d_].rearrange("h k v -> k h v"), (), [sg.r(2 + (1 - d_))], "sti%d" % d_)
    identf = identF[:, 0, :]
    cp(idb[:, 0, :], identf, [identF.r()], [idb.r()])
    memset(onesb[:, 0, :], 1.0, [onesb.r()])
    cp(wsTb[:, 0, :], wsTf[:, 0, :], [wsTf.r()], [wsTb.r()])
    act(ecol[:, 0, :], rows[:, 0, R_DL:R_DL + 8], AF.Exp, [rows.r()], [ecol.r()], scale=-1.0)
    act(nlg[:, 0, :], ecol[:, 0, :], AF.Ln, [ecol.r()], [nlg.r()], bias=1.0)
    for h in range(4):
        nf = nlg[:, 0, h:h + 1]
        nb = nlg[:, 0, 4 + h:5 + h]
        act(DT[:, h, :], cstb[:, 0, C_NA:C_NA + 128], AF.Exp, [cstb.r(), nlg.r()], [DT.r(h)], scale=nf, bias=QSCALE_LN)
        tt(DT[:, h, :], DT[:, h, :], cstb[:, 0, C_M1:C_M1 + 128], ALU.mult, [DT.r(h), cstb.r()], [DT.r(h)])
        act(dtmp[:, 0, :], cstb[:, 0, C_NB:C_NB + 128], AF.Exp, [cstb.r(), nlg.r()], [dtmp.r()], scale=nb, bias=QSCALE_LN)
        tt(dtmp[:, 0, :], dtmp[:, 0, :], cstb[:, 0, C_M2:C_M2 + 128], ALU.mult, [dtmp.r(), cstb.r()], [dtmp.r()])
        tt(DT[:, h, :], DT[:, h, :], dtmp[:, 0, :], ALU.add, [DT.r(h), dtmp.r()], [DT.r(h)])
        hs = slice(h * 128, (h + 1) * 128)
        act(crossf[:, 0, hs], cstb[:, 0, C_NI1:C_NI1 + 128], AF.Exp, [cstb.r(), nlg.r()], [crossf.r()], scale=nf, bias=QSCALE_LN)
        act(crossb[:, 0, hs], cstb[:, 0, C_NI2:C_NI2 + 128], AF.Exp, [cstb.r(), nlg.r()], [crossb.r()], scale=nb, bias=QSCALE_LN)
        act(kcol[:, 0, h:h + 1], cstb[:, 0, C_PC:C_PC + 1], AF.Exp, [cstb.r(), nlg.r()], [kcol.r()], scale=nf)
        act(kcol[:, 0, 4 + h:5 + h], cstb[:, 0, C_PC + 1:C_PC + 2], AF.Exp, [cstb.r(), nlg.r()], [kcol.r()], scale=nb)
        act(kcol[:, 0, 8 + h:9 + h], nf, AF.Exp, [nlg.r()], [kcol.r()], scale=-128.0)
        act(kcol[:, 0, 12 + h:13 + h], nb, AF.Exp, [nlg.r()], [kcol.r()], scale=-128.0)
    for h in range(4):
        hs = slice(h * 128, (h + 1) * 128)
        for tab, c in ((kdecf, h), (kdecb, 4 + h), (cdecf, 8 + h), (cdecb, 12 + h)):
            cp(tab[:, 0, hs], kcol[:, 0, c:c + 1].to_broadcast([128, 128]), [kcol.r()], [tab.r()])

    if stage == 1:
        return nc, S.emit(nc)
    castn = [0]

    def cast_units(u0, u1, gate=False):
        castn[0] += 1
        dma("pool", wbf.t[u0:u1].rearrange("u p f -> p u f"), wall[u0:u1].rearrange("u p f -> p u f"),
            [("gate", 0, 1)] if gate else (), [wbf.r(u0, u1)], "scr%d" % castn[0])

    for u in (1, 2, 4, 5):
        cast_units(u, u + 1)

    act(scT[:, 0, :, 0], vec[:, 0, V_C:V_C + 8], AF.Silu, [vec.r()], [scT.r()])
    act(scT[:, 0, :, 1], vec[:, 0, V_CCTX:V_CCTX + 8], AF.Silu, [vec.r()], [scT.r()])
    psA = pnext()
    ci = 0
    for l in range(2):
        for cc in range(12):
            slot = ci % R
            ci += 1
            dma("pool", ring[:, slot, :].rearrange("p (k n) -> p k n", k=8),
                adaw[l, :, cc * 512:(cc + 1) * 512].rearrange("(k p) n -> p k n", p=128), (), [ring.r(slot)], "ada%d" % slot)
            for j in range(4):
                mi = l * 48 + cc * 4 + j
                for kc in range(8):
                    mm(psA[:, 2 * mi:2 * mi + 2], ring[:, slot, kc * 512 + j * 128:kc * 512 + (j + 1) * 128], scT[:, 0, kc, :],
                       kc == 0, kc == 7, [ring.r(slot), scT.r()], [psA.r()])
    tt(modT[:, 0, :].rearrange("p (a b) -> p a b", b=2), psA[:, 0:192].rearrange("p (a b) -> p a b", b=2),
       vec[:, 0, V_ADAB:V_ADAB + 96].unsqueeze(2).to_broadcast([128, 96, 2]), ALU.add, [psA.r(), vec.r()], [modT.r()])

    def mcol(l, part, fc, cond):
        i = ((l * 6 + part) * 8 + fc) * 2 + cond
        return modT[:, 0, i:i + 1]

    modT5 = modT[:, 0, :].rearrange("p (l q f c) -> p l q f c", l=2, q=6, f=8)
    for l in range(2):
        for w in range(2):
            for cond in range(2):
                gi = ((l * 2 + w) * 2 + cond) * 8
                gv = V_NMG if w == 0 else V_NFG
                ts(Gt[:, 0, gi:gi + 8], modT5[:, l, 1 + 3 * w, :, cond], 1.0, None, ALU.add, None, [modT.r()], [Gt.r()])
                tt(Gt[:, 0, gi:gi + 8], Gt[:, 0, gi:gi + 8], vec[:, 0, gv + l * 8:gv + l * 8 + 8], ALU.mult, [Gt.r(), vec.r()], [Gt.r()])

    def gcol(l, w, cond, fc):
        i = ((l * 2 + w) * 2 + cond) * 8 + fc
        return Gt[:, 0, i:i + 1]

    def cast_rest():
        rest = [u for u in range(NUNIT) if u not in (1, 2, 4, 5)]
        for i in range(0, len(rest), 4):
            grp = rest[i:i + 4]
            if grp[-1] - grp[0] == len(grp) - 1:
                cast_units(grp[0], grp[-1] + 1, gate=True)
            else:
                for u in grp:
                    cast_units(u, u + 1, gate=True)

    if stage == 2:
        return nc, S.emit(nc)
    def tile_info(t):
        if t < NS:
            return dict(cond=0, rope=True, xd=xs, yd=ys, row0=t * 512, sample=True, lt=t)
        return dict(cond=1, rope=False, xd=xp, yd=yp, row0=(t - NS) * 512, sample=False, lt=t - NS)

    def chunk_info(t, n):
        if t < NS:
            return dict(first=(t == 0 and n == 0), last=(t == NS - 1 and n == 3), seq=None)
        return dict(first=(n % 2 == 0), last=(n % 2 == 1), seq=2 * (t - NS) + n // 2)

    xcnt = [0]
    ocnt = [0]

    xpref = {}

    def x_dma(t, n):
        ti = tile_info(t)
        s = xcnt[0] % 4
        xcnt[0] += 1
        dma("sp", xin[:, s, :], ti["xd"][ti["row0"] + n * 128:ti["row0"] + (n + 1) * 128, :], (), [xin.r(s)], "xin%d" % s)
        return s

    def prefetch_x(t):
        if 0 <= t < NT and t not in xpref:
            xpref[t] = [x_dma(t, n) for n in range(4)]

    def load_xT(t):
        S.tag = 'loadx'
        pre = xpref.pop(t, None)
        for n in range(4):
            if pre is not None:
                s = pre[n]
            else:
                s = x_dma(t, n)
            for half in range(2):
                pb = pnext()
                for j in range(4):
                    fc = half * 4 + j
                    tr(pb[:, j * 128:(j + 1) * 128], xin[:, s, fc * 128:(fc + 1) * 128], identf, [xin.r(s), identF.r()], [pb.r()])
                S.add("act", lambda e, pb=pb, half=half, n=n: e.activation(
                    out=xT[:, half * 4:half * 4 + 4, n * 128:(n + 1) * 128], in_=pb[:, :].rearrange("p (a b) -> p a b", a=4), func=AF.Copy),
                    [pb.r()], [xT.r(half * 4, half * 4 + 4)])
        for fc in range(8):
            sq_acc(fc)
            sq_step()

    ncnt = [0]

    sqc = [0]

    sq_pend = []

    def sq_acc(fc):
        sl = sqc[0] % 4
        sqc[0] += 1
        act(sqb[:, sl, :], xT[:, fc, :], AF.Square, [xT.r(fc)], [sqb.r(sl)])
        assert len(sq_pend) < 4
        sq_pend.append([2, lambda: mm(SSB[:, :], onesb[:, 0, :], sqb[:, sl, :], fc == 0, fc == 7, [onesb.r(), sqb.r(sl)], [SSB.r()])])

    def sq_step(flush=False):
        while sq_pend and (flush or sq_pend[0][0] <= 0):
            sq_pend.pop(0)[1]()
        for p in sq_pend:
            p[0] -= 1

    def rsqrt_from(dst, dst_r, src, src_r, scale):
        act(dst, src, AF.Ln, [src_r], [dst_r], bias=EPS, scale=scale)
        act(dst, dst, AF.Exp, [dst_r], [dst_r], scale=-0.5)

    def norm_mod(l, w, cond, gain=None):
        sq_step(flush=True)
        S.tag = 'norm'
        rs = ncnt[0] % 2
        ncnt[0] += 1
        rsqrt_from(rbuf[:, rs, :], rbuf.r(rs), SSB[:, :], SSB.r(), 1.0 / 1024.0)
        pbw = pnext()
        for _ in range(NWARM):
            mm(pbw[:, :], onesb[:, 0, :], wsTb[:, 0, :], True, True, [onesb.r(), wsTb.r()], [pbw.r()])
        return rs

    def apply_mod(rs, l, w, cond):
        for fc in range(8):
            s = (fc // 2) % 2
            if fc % 2 == 0:
                tt(tmpf[:, s, :], xT[:, fc, :], rbuf[:, rs, :], ALU.mult, [xT.r(fc), rbuf.r(rs)], [tmpf.r(s)])
                act(hT[:, fc, :], tmpf[:, s, :], AF.Identity, [tmpf.r(s), Gt.r(), modT.r()], [hT.r(fc)],
                    bias=mcol(l, 3 * w, fc, cond), scale=gcol(l, w, cond, fc))
            else:
                stt(tmpf[:, 2 + s, :], xT[:, fc, :], gcol(l, w, cond, fc), rbuf[:, rs, :], ALU.mult, ALU.mult,
                    [xT.r(fc), Gt.r(), rbuf.r(rs)], [tmpf.r(2 + s)])
                ts(hT[:, fc, :], tmpf[:, 2 + s, :], mcol(l, 3 * w, fc, cond), None, ALU.add, None, [tmpf.r(2 + s), modT.r()], [hT.r(fc)])

    def rope_evac(pb, dst, slot_ap_r, n_local, post_tab=None, dst_r=None):
        rsl, rr = slot_ap_r
        cosb = ropeb[:, rsl, 0 + n_local, :]
        sinb = ropeb[:, rsl, 4 + n_local, :]
        nsinb = ropeb[:, rsl, 8 + n_local, :]
        s = 0
        pb4 = pb[:, :].rearrange("p (h a d) -> p h a d", h=4, a=2)
        ta4 = qkf[:, s, :].rearrange("p (h a d) -> p h a d", h=4, a=2)
        tb4 = rtb[:, s, :].rearrange("p (h a d) -> p h a d", h=4, a=2)
        tt(ta4, pb4, cosb.unsqueeze(1).unsqueeze(1).to_broadcast([128, 4, 2, 64]), ALU.mult, [pb.r(), rr], [qkf.r(s)])
        tt(tb4[:, :, 0, :], pb4[:, :, 1, :], nsinb.unsqueeze(1).to_broadcast([128, 4, 64]), ALU.mult, [pb.r(), rr], [rtb.r(s)])
        tt(tb4[:, :, 1, :], pb4[:, :, 0, :], sinb.unsqueeze(1).to_broadcast([128, 4, 64]), ALU.mult, [pb.r(), rr], [rtb.r(s)])
        if post_tab is None:
            tt(dst, qkf[:, s, :], rtb[:, s, :], ALU.add, [qkf.r(s), rtb.r(s)], [dst_r])
        else:
            tt(qkf[:, s, :], qkf[:, s, :], rtb[:, s, :], ALU.add, [qkf.r(s), rtb.r(s)], [qkf.r(s)])
        return s

    rope_cnt = [0]
    ropeload = [0]

    def load_rope(t):
        rsl = ropeload[0] % 2
        ropeload[0] += 1
        dma("sp", ropeb[:, rsl, :, :], rope_d[t].rearrange("p a n d -> p (a n) d"), (), [ropeb.r(rsl)], "rope%d" % rsl)
        return rsl

    def init_state(buf, d, ci):
        if ci["seq"] is None:
            cp(buf[:, 0, :], sg[:, 2 + (1 - d), :], [sg.r(2 + (1 - d))], [buf.r()])
        else:
            memset(buf[:, 0, :], 0.0, [buf.r()])

    for slot, u in enumerate((1, 2, 4, 5)):
        dma("sp", ring[:, slot, :], wbf.t[u], [wbf.r(u)], [ring.r(slot)], "w%d" % slot)
    gate_t = max(NT - 4, 0)
    for t in range(NT - 1, -1, -1):
        ti = tile_info(t)
        cond = ti["cond"]
        if ti["rope"]:
            rsl = load_rope(ti["lt"])
        load_xT(t)
        prefetch_x(t - 1)
        if t == gate_t:
            memset(hcg[:, 0, 0:1], 0.0, [("gate", 0, 1), hcg.r()])
            cast_rest()
        rs = norm_mod(0, 0, cond)
        apply_mod(rs, 0, 0, cond)
        ph = pnext()
        for m in range(8):
            slot = 2 + m // 4
            mc = m % 4
            for kc in range(8):
                mm(ph[:, 2 * m:2 * m + 2], ring[:, slot, (mc * 8 + kc) * 128:(mc * 8 + kc + 1) * 128], hT[:, kc, 0:512:511],
                   kc == 0, kc == 7, [ring.r(slot), hT.r(kc)], [ph.r()])
        cp(hcg[:, 0, :], ph[:, 0:8], [ph.r()], [hcg.r()])
        tt(zh[:, 0, t, :], ph[:, 8:16], hcg[:, 0, :], ALU.mult, [ph.r(), hcg.r()], [zh.r()])
        for n in range(3, -1, -1):
            ci = chunk_info(t, n)
            pk = pnext()
            pv = pnext()
            for kc in range(8):
                mm(pk[:, :], hT[:, kc, n * 128:(n + 1) * 128], ring[:, 0, kc * 512:(kc + 1) * 512], kc == 0, kc == 7, [hT.r(kc), ring.r(0)], [pk.r()])
            for kc in range(8):
                mm(pv[:, :], hT[:, kc, n * 128:(n + 1) * 128], ring[:, 1, kc * 512:(kc + 1) * 512], kc == 0, kc == 7, [hT.r(kc), ring.r(1)], [pv.r()])
            if ti["rope"]:
                s = rope_evac(pk, None, (rsl, ropeb.r(rsl)), n, post_tab=True)
                act(k_tm[:, n, :], qkf[:, s, :], AF.Copy, [qkf.r(s)], [k_tm.r(n)])
                tt(kdf[:, n, :], qkf[:, s, :], kdecb[:, 0, :], ALU.mult, [qkf.r(s), kdecb.r()], [kdf.r(n)])
            else:
                act(k_tm[:, n, :], pk[:, :], AF.Copy, [pk.r()], [k_tm.r(n)])
                tt(kdf[:, n, :], pk[:, :], kdecb[:, 0, :], ALU.mult, [pk.r(), kdecb.r()], [kdf.r(n)])
            act(v_tm[:, n, :], pv[:, :], AF.Copy, [pv.r()], [v_tm.r(n)])
            pU = pnext()
            for h in range(4):
                hs = slice(h * 128, (h + 1) * 128)
                mm(pU[:, hs], kdf[:, n, hs], v_tm[:, n, hs], True, True, [kdf.r(n), v_tm.r(n)], [pU.r()])
            if ci["last"]:
                scur[0] ^= 1
                init_state(Srun[scur[0]], 1, ci)
            Sb_run = Srun[scur[0]]
            cp(sbst[:, n, :], Sb_run[:, 0, :], [Sb_run.r()], [sbst.r(n)])
            tt(Sb_run[:, 0, :], Sb_run[:, 0, :], cdecb[:, 0, :], ALU.mult, [Sb_run.r(), cdecb.r()], [Sb_run.r()])
            tt(Sb_run[:, 0, :], Sb_run[:, 0, :], pU[:, :], ALU.add, [Sb_run.r(), pU.r()], [Sb_run.r()])
            if ci["first"] and ci["seq"] is not None:
                dma("sp", nst[ci["seq"], 1].rearrange("h k v -> k h v"), Sb_run[:, 0, :].rearrange("p (h v) -> p h v", h=4),
                    [Sb_run.r()], (), "nst")
        dma("sp", sbscr.t[t], sbst[:, :, :], [sbst.r()], [sbscr.r(t)], "sbst")
        dma("sp", kvscr.t[t, :, 0:4, :], k_tm[:, :, :], [k_tm.r()], [kvscr.r(t)], "kvst")
        dma("sp", kvscr.t[t, :, 4:8, :], v_tm[:, :, :], [v_tm.r()], [kvscr.r(t)], "kvst")

    if stage == 3:
        return nc, S.emit(nc, final_chans=['nst', 'sbst'])
    wpos = [0]
    wloaded = [0]
    total_units = NT * NUNIT

    def w_ensure():
        while wloaded[0] < min(wpos[0] // 32 + R, total_units):
            g = wloaded[0]
            slot = g % R
            dma("sp", ring[:, slot, :], wbf.t[g % NUNIT], [wbf.r(g % NUNIT)], [ring.r(slot)], "w%d" % slot)
            wloaded[0] += 1

    def wpeek(off, n):
        p = wpos[0] + off
        u = p // 32
        b = p % 32
        assert b + n <= 32 and u < wpos[0] // 32 + R
        w_ensure()
        slot = u % R
        return ring[:, slot, b * 128:(b + n) * 128], ring.r(slot)

    def wadv(n):
        wpos[0] += n
        w_ensure()

    WST = weight_stream()

    def wcheck(mat, l, kc, cc, off=0):
        assert WST[(wpos[0] + off) % 1536] == (mat, l, kc, cc), (WST[(wpos[0] + off) % 1536], (mat, l, kc, cc))

    def proj_fm(mat, l, mc_list, n_k, rhs_fn, evac):
        for mc in mc_list:
            pb = pnext()
            for kc in range(n_k):
                wcheck(mat, l, kc, mc)
                wap, wr = wpeek(0, 1)
                rhs, rr = rhs_fn(kc)
                mm(pb[:, :], wap, rhs, kc == 0, kc == n_k - 1, [wr, rr], [pb.r()])
                wadv(1)
            sq_step()
            evac(mc, pb)

    def prework_pieces(tn):
        pre = xpref.pop(tn)
        pieces = []
        pend = []

        def step_pend(flush=False):
            while pend and (flush or pend[0][0] <= 0):
                pend.pop(0)[1]()
            for p in pend:
                p[0] -= 1

        for n in range(4):
            for half in range(2):
                def p(n=n, half=half):
                    pb = pnext()
                    for j in range(4):
                        fc = half * 4 + j
                        tr(pb[:, j * 128:(j + 1) * 128], xin[:, pre[n], fc * 128:(fc + 1) * 128], identf, [xin.r(pre[n]), identF.r()], [pb.r()])
                    S.add("act", lambda e: e.activation(
                        out=xTn[:, half * 4:half * 4 + 4, n * 128:(n + 1) * 128], in_=pb[:, :].rearrange("p (a b) -> p a b", a=4), func=AF.Copy),
                        [pb.r()], [xTn.r(half * 4, half * 4 + 4)])
                    if n == 3 and half == 1:
                        prefetch_x(tn + 1)
                pieces.append(p)
        for fc in range(8):
            def q(fc=fc):
                sl = sqc[0] % 4
                sqc[0] += 1
                act(sqb[:, sl, :], xTn[:, fc, :], AF.Square, [xTn.r(fc)], [sqb.r(sl)])
                pend.append([2, lambda: mm(SSB[:, :], onesb[:, 0, :], sqb[:, sl, :], fc == 0, fc == 7, [onesb.r(), sqb.r(sl)], [SSB.r()])])
                step_pend()
            pieces.append(q)
        pieces.append(lambda: step_pend())
        pieces.append(lambda: step_pend(flush=True))
        return pieces

    def ffn(l, cond, pre_tile=None):
        rs = norm_mod(l, 1, cond)
        apply_mod(rs, l, 1, cond)
        S.tag = 'ffn_gu'
        pieces = prework_pieces(pre_tile) if pre_tile is not None else []
        for m in range(22):
            pg = pnext()
            pu = pnext()
            for kc in range(8):
                for pb, mat in ((pg, "w_gate"), (pu, "w_up")):
                    wcheck(mat, l, kc, m)
                    wap, wr = wpeek(0, 1)
                    mm(pb[:, :], wap, hT[:, kc, :], kc == 0, kc == 7, [wr, hT.r(kc)], [pb.r()])
                    wadv(1)
            s = m % 2
            act(sgt[:, s, :], pg[:, :], AF.Silu, [pg.r()], [sgt.r(s)])
            tt(hidT[:, m, :], sgt[:, s, :], pu[:, :], ALU.mult, [sgt.r(s), pu.r()], [hidT.r(m)])
            if pieces and m >= 1:
                pieces.pop(0)()
        while pieces:
            pieces.pop(0)()
        if pre_tile is not None:
            cn = tile_info(pre_tile)["cond"]
            rsqrt_from(rb2[:, 0, :], rb2.r(), SSB[:, :], SSB.r(), 1.0 / 1024.0)

        def ev(mc, pb):
            stt(xT[:, mc, :], pb[:, :], mcol(l, 5, mc, cond), xT[:, mc, :], ALU.mult, ALU.add, [pb.r(), modT.r(), xT.r(mc)], [xT.r(mc)])
            sq_acc(mc)
            if pre_tile is not None:
                fc = mc
                s_ = (fc // 2) % 2
                if fc % 2 == 0:
                    tt(tmpf[:, s_, :], xTn[:, fc, :], rb2[:, 0, :], ALU.mult, [xTn.r(fc), rb2.r()], [tmpf.r(s_)])
                    act(hT[:, fc, :], tmpf[:, s_, :], AF.Identity, [tmpf.r(s_), Gt.r(), modT.r()], [hT.r(fc)],
                        bias=mcol(0, 0, fc, cn), scale=gcol(0, 0, cn, fc))
                else:
                    stt(tmpf[:, 2 + s_, :], xTn[:, fc, :], gcol(0, 0, cn, fc), rb2[:, 0, :], ALU.mult, ALU.mult,
                        [xTn.r(fc), Gt.r(), rb2.r()], [tmpf.r(2 + s_)])
                    ts(hT[:, fc, :], tmpf[:, 2 + s_, :], mcol(0, 0, fc, cn), None, ALU.add, None, [tmpf.r(2 + s_), modT.r()], [hT.r(fc)])
        S.tag = 'ffn_dn'
        proj_fm("w_down", l, range(8), 22, lambda kc: (hidT[:, kc, :], hidT.r(kc)), ev)

    preworked = set()

    def chk(k):
        if stage == k:
            raise StopBuild()

    try:
      for t in range(NT):
        ti = tile_info(t)
        cond = ti["cond"]
        if ti["rope"]:
            rsl = load_rope(ti["lt"])
        dma("sp", SbL[:, :, :], sbscr.t[t], [sbscr.r(t)], [SbL.r()], "sbl")
        dma("sp", k_tm[:, :, :], kvscr.t[t, :, 0:4, :], [kvscr.r(t)], [k_tm.r()], "kvl")
        dma("sp", v_tm[:, :, :], kvscr.t[t, :, 4:8, :], [kvscr.r(t)], [v_tm.r()], "kvl")
        if t in preworked:
            S.tag = 'loadx'
            for fc in range(8):
                if fc % 2 == 0:
                    act(xT[:, fc, :], xTn[:, fc, :], AF.Copy, [xTn.r(fc)], [xT.r(fc)])
                else:
                    cp(xT[:, fc, :], xTn[:, fc, :], [xTn.r(fc)], [xT.r(fc)])
        else:
            load_xT(t)
            prefetch_x(t + 1)
            rs = norm_mod(0, 0, cond)
            apply_mod(rs, 0, 0, cond)
        S.tag = 'qkv'
        def qkv_evac(grp, n, pb):
            if grp == 0:
                if ti["rope"]:
                    rope_evac(pb, q_tm[:, n, :], (rsl, ropeb.r(rsl)), n, dst_r=q_tm.r(n))
                else:
                    act(q_tm[:, n, :], pb[:, :], AF.Copy, [pb.r()], [q_tm.r(n)])
            elif grp == 1:
                if ti["rope"]:
                    s_ = rope_evac(pb, None, (rsl, ropeb.r(rsl)), n, post_tab=True)
                    act(k_tm[:, n, :], qkf[:, s_, :], AF.Copy, [qkf.r(s_)], [k_tm.r(n)])
                    tt(kdf[:, n, :], qkf[:, s_, :], kdecf[:, 0, :], ALU.mult, [qkf.r(s_), kdecf.r()], [kdf.r(n)])
                else:
                    act(k_tm[:, n, :], pb[:, :], AF.Copy, [pb.r()], [k_tm.r(n)])
                    tt(kdf[:, n, :], pb[:, :], kdecf[:, 0, :], ALU.mult, [pb.r(), kdecf.r()], [kdf.r(n)])
            else:
                act(v_tm[:, n, :], pb[:, :], AF.Copy, [pb.r()], [v_tm.r(n)])

        for grp in range(3):
            if grp == 0:
                pbs = [pnext() for _ in range(4)]
                for kc in range(8):
                    wcheck("w_in_ab", 0, kc, grp * 4, off=kc * 4)
                    wap, wr = wpeek(kc * 4, 4)
                    for n in range(4):
                        mm(pbs[n][:, :], hT[:, kc, n * 128:(n + 1) * 128], wap, kc == 0, kc == 7, [hT.r(kc), wr], [pbs[n].r()])
                for n in range(4):
                    qkv_evac(grp, n, pbs[n])
            elif grp == 1:
                for n in range(4):
                    tt(kdf[:, n, :], k_tm[:, n, :], kdecf[:, 0, :], ALU.mult, [k_tm.r(n), kdecf.r()], [kdf.r(n)])
            wadv(32)
        chk(41)
        S.tag = 'gconv'
        hrhs = lambda kc: (hT[:, kc, :], hT.r(kc))

        if ti["sample"]:
            if t > 0:
                cp(zb[:, :, 0:1], zh[:, 0, t - 1, 1:8:2].unsqueeze(2), [zh.r()], [zb.r()])
            else:
                memset(zb[:, :, 0:1], 0.0, [zb.r()])
            if t < NS - 1:
                cp(zb[:, :, 513:514], zh[:, 0, t + 1, 0:8:2].unsqueeze(2), [zh.r()], [zb.r()])
            else:
                memset(zb[:, :, 513:514], 0.0, [zb.r()])
            segs = None
        else:
            segs = [(0, 256), (256, 512)]

        def conv_mc(mc):
            w0 = vec[:, 0, V_CONVW + mc:V_CONVW + mc + 1]
            w1 = vec[:, 0, V_CONVW + 4 + mc:V_CONVW + 5 + mc]
            w2 = vec[:, 0, V_CONVW + 8 + mc:V_CONVW + 9 + mc]
            cb = vec[:, 0, V_CONVB + mc:V_CONVB + mc + 1]
            act(cgt[:, mc, :], zb[:, mc, 1:513], AF.Identity, [zb.r(mc), vec.r()], [cgt.r(mc)], bias=cb, scale=w1)
            if segs is None:
                stt(cgt[:, mc, :], zb[:, mc, 0:512], w0, cgt[:, mc, :], ALU.mult, ALU.add, [zb.r(mc), vec.r(), cgt.r(mc)], [cgt.r(mc)])
                stt(cgt[:, mc, :], zb[:, mc, 2:514], w2, cgt[:, mc, :], ALU.mult, ALU.add, [zb.r(mc), vec.r(), cgt.r(mc)], [cgt.r(mc)])
            else:
                for (a, b) in segs:
                    stt(cgt[:, mc, a + 1:b], zb[:, mc, a + 1:b], w0, cgt[:, mc, a + 1:b], ALU.mult, ALU.add, [zb.r(mc), vec.r(), cgt.r(mc)], [cgt.r(mc)])
                    stt(cgt[:, mc, a:b - 1], zb[:, mc, a + 2:b + 1], w2, cgt[:, mc, a:b - 1], ALU.mult, ALU.add, [zb.r(mc), vec.r(), cgt.r(mc)], [cgt.r(mc)])

        fillers = []
        for i in range(4):
            fillers.append(lambda i=i: proj_fm("w_in_ab", 0, [12 + i], 8, hrhs,
                           lambda mc, pb: act(sg[:, mc - 12, :], pb[:, :], AF.Silu, [pb.r()], [sg.r(mc - 12)])))
        for i in range(4):
            fillers.append(lambda i=i: proj_fm("w_in_ab", 0, [20 + i], 8, hrhs,
                           lambda mc, pb: act(cgt[:, mc - 20, :], pb[:, :], AF.Copy, [pb.r()], [cgt.r(mc - 20)])))
        for i in range(4):
            def f_xc(i=i):
                proj_fm("w_in_ab", 0, [24 + i], 8, hrhs,
                        lambda mc, pb: tt(zb[:, mc - 24, 1:513], pb[:, :], cgt[:, mc - 24, :], ALU.mult, [pb.r(), cgt.r(mc - 24)], [zb.r(mc - 24)]))
                conv_mc(i)
            fillers.append(f_xc)
        for i in range(4):
            fillers.append(lambda i=i: proj_fm("w_in_ab", 0, [16 + i], 8, hrhs,
                           lambda mc, pb: tt(yT[:, 4 + mc - 16, :], pb[:, :], cgt[:, mc - 16, :], ALU.mult, [pb.r(), cgt.r(mc - 16)], [yT.r(4 + mc - 16)])))
        fi = [0]

        def fill():
            if fi[0] < len(fillers):
                tg = S.tag
                S.tag = 'gconv'
                fillers[fi[0]]()
                fi[0] += 1
                S.tag = tg
        fill()
        fill()
        chk(4)
        S.tag = 'state'
        for n in range(4):
            ci = chunk_info(t, n)
            if ci["first"]:
                scur[0] ^= 1
                init_state(Srun[scur[0]], 0, ci)
            Sf_run = Srun[scur[0]]
            cp(SfB[:, n, :], Sf_run[:, 0, :], [Sf_run.r()], [SfB.r(n)])
            pU = pnext()
            for h in range(4):
                hs = slice(h * 128, (h + 1) * 128)
                mm(pU[:, hs], kdf[:, n, hs], v_tm[:, n, hs], True, True, [kdf.r(n), v_tm.r(n)], [pU.r()])
            tt(Sf_run[:, 0, :], Sf_run[:, 0, :], cdecf[:, 0, :], ALU.mult, [Sf_run.r(), cdecf.r()], [Sf_run.r()])
            tt(Sf_run[:, 0, :], Sf_run[:, 0, :], pU[:, :], ALU.add, [Sf_run.r(), pU.r()], [Sf_run.r()])
            if ci["last"] and ci["seq"] is not None:
                dma("sp", nst[ci["seq"], 0].rearrange("h k v -> k h v"), Sf_run[:, 0, :].rearrange("p (h v) -> p h v", h=4),
                    [Sf_run.r()], (), "nst")
        chk(5)
        S.tag = 'ret'
        v4 = lambda ap: ap.rearrange("p (a b) -> p a b", a=4)
        hstate = {}

        def step_T(h):
            hs = slice(h * 128, (h + 1) * 128)
            s = h % 2
            for n in range(4):
                ns = slice(n * 128, (n + 1) * 128)
                tr(ptb_t[:, 0, ns], q_tm[:, n, hs], idb[:, 0, :], [q_tm.r(n), idb.r()], [ptr(0)])
            for n in range(4):
                ns = slice(n * 128, (n + 1) * 128)
                tr(ptb_t[:, 1, ns], k_tm[:, n, hs], idb[:, 0, :], [k_tm.r(n), idb.r()], [ptr(1)])
            act(qT[:, s, :], ptb_t[:, 0, :], AF.Copy, [ptr(0)], [qT.r(s)])
            bc = lambda tab: tab[:, 0, hs].unsqueeze(1).to_broadcast([128, 4, 128])
            tt(v4(qfT[:, s, :]), v4(ptb_t[:, 0, :]), bc(crossf), ALU.mult, [ptr(0), crossf.r()], [qfT.r(s)])
            tt(v4(qbT[:, s, :]), v4(ptb_t[:, 0, :]), bc(crossb), ALU.mult, [ptr(0), crossb.r()], [qbT.r(s)])
            act(kT[:, s, :], ptb_t[:, 1, :], AF.Copy, [ptr(1)], [kT.r(s)])

        def step_S(h):
            s = h % 2
            psc = pnext()
            for n in range(4):
                ns = slice(n * 128, (n + 1) * 128)
                mm(psc[:, ns], kT[:, s, ns], qT[:, s, ns], True, True, [kT.r(s), qT.r(s)], [psc.r()])
            tt(v4(scm[:, s, :]), v4(psc[:, :]), DT[:, h, :].unsqueeze(1).to_broadcast([128, 4, 128]), ALU.mult, [psc.r(), DT.r(h)], [scm.r(s)])

        def step_O(h):
            hs = slice(h * 128, (h + 1) * 128)
            s = h % 2
            po = pnext()
            for n in range(4):
                ns = slice(n * 128, (n + 1) * 128)
                mm(po[:, ns], v_tm[:, n, hs], scm[:, s, ns], True, False, [v_tm.r(n), scm.r(s)], [po.r()])
                mm(po[:, ns], SfB[:, n, hs], qfT[:, s, ns], False, False, [SfB.r(n), qfT.r(s)], [po.r()])
                mm(po[:, ns], SbL[:, n, hs], qbT[:, s, ns], False, True, [SbL.r(n), qbT.r(s)], [po.r()])
            act(o_sb[:, s, :], po[:, :], AF.Copy, [po.r()], [o_sb.r(s)])
            act(sqb[:, s, :], po[:, :], AF.Square, [po.r()], [sqb.r(s)])

        def step_N(h):
            s = h % 2
            pss = pnext()
            mm(pss[:, :], onesb[:, 0, :], sqb[:, s, :], True, True, [onesb.r(), sqb.r(s)], [pss.r()])
            rsqrt_from(rbuf[:, s, :], rbuf.r(s), pss[:, :], pss.r(), 1.0 / 128.0)
            tt(o_sb[:, s, :], o_sb[:, s, :], rbuf[:, s, :], ALU.mult, [o_sb.r(s), rbuf.r(s)], [o_sb.r(s)])
            stt(yT[:, h, :], o_sb[:, s, :], vec[:, 0, V_RETG + h:V_RETG + h + 1], sg[:, h, :], ALU.mult, ALU.mult,
                [o_sb.r(s), vec.r(), sg.r(h)], [yT.r(h)])

        for (a, b) in ((0, 1), (2, 3)):
            step_T(a)
            fill()
            step_T(b)
            fill()
            step_S(a)
            fill()
            step_S(b)
            fill()
            step_O(a)
            step_O(b)
            fill()
            step_N(a)
            step_N(b)
            fill()
        while fi[0] < len(fillers):
            fill()
        chk(6)
        S.tag = 'outab'
        def ev_ab(mc, pb):
            stt(xT[:, mc, :], pb[:, :], mcol(0, 2, mc, cond), xT[:, mc, :], ALU.mult, ALU.add, [pb.r(), modT.r(), xT.r(mc)], [xT.r(mc)])
            sq_acc(mc)
        proj_fm("w_out_ab", 0, range(8), 8, lambda kc: (yT[:, kc, :], yT.r(kc)), ev_ab)
        ffn(0, cond)
        chk(7)
        S.tag = 'l1'
        rs = norm_mod(1, 0, cond)
        apply_mod(rs, 1, 0, cond)
        for half in range(2):
            pbs = [pnext() for _ in range(4)]
            for kc in range(8):
                wcheck("w_in_c", 0, kc, 8 + half * 4, off=half * 32 + kc * 4)
                wap, wr = wpeek(half * 32 + kc * 4, 4)
                for n in range(4):
                    mm(pbs[n][:, :], hT[:, kc, n * 128:(n + 1) * 128], wap, kc == 0, kc == 7, [hT.r(kc), wr], [pbs[n].r()])
            for n in range(4):
                act(vg[:, n, half * 512:(half + 1) * 512], pbs[n][:, :], AF.Gelu_apprx_tanh, [pbs[n].r()], [vg.r(n)])
        for n in range(4):
            act(vsq[:, 0, :], vg[:, n, :], AF.Square, [vg.r(n)], [vsq.r(), rv.r()], accum=rv[:, 0, n:n + 1])
        rsqrt_from(rv[:, 0, 4:8], rv.r(), rv[:, 0, 0:4], rv.r(), 1.0 / 1024.0)
        for n in range(4):
            stt(vn[:, n, :], vg[:, n, :], rv[:, 0, 4 + n:5 + n], rows[:, 0, R_CNG:R_CNG + 1024], ALU.mult, ALU.mult,
                [vg.r(n), rv.r(), rows.r()], [vn.r(n)])
        wadv(64)
        proj_fm("w_in_c", 0, range(8), 8, hrhs,
                lambda mc, pb: act(uT[:, mc, :], pb[:, :], AF.Gelu_apprx_tanh, [pb.r()], [uT.r(mc)]))
        for cc in range(8):
            g = cc // 2
            pb = pnext()
            for n in range(4):
                ns = slice(n * 128, (n + 1) * 128)
                mm(pb[:, ns], vn[:, n, cc * 128:(cc + 1) * 128], wsTb[:, 0, g * 128:(g + 1) * 128], True, True, [vn.r(n), wsTb.r()], [pb.r()])
            s = cc % 2
            tt(stt_[:, s, :].rearrange("p (a b) -> p a b", a=4), pb[:, :].rearrange("p (a b) -> p a b", a=4),
               rows[:, 0, R_BS + g * 128:R_BS + (g + 1) * 128].unsqueeze(1).to_broadcast([128, 4, 128]), ALU.add, [pb.r(), rows.r()], [stt_.r(s)])
            tt(mT[:, cc, :], stt_[:, s, :], uT[:, cc, :], ALU.mult, [stt_.r(s), uT.r(cc)], [mT.r(cc)])
        def ev_c(mc, pb):
            stt(xT[:, mc, :], pb[:, :], mcol(1, 2, mc, cond), xT[:, mc, :], ALU.mult, ALU.add, [pb.r(), modT.r(), xT.r(mc)], [xT.r(mc)])
            sq_acc(mc)
        proj_fm("w_out_c", 0, range(8), 8, lambda kc: (mT[:, kc, :], mT.r(kc)), ev_c)
        if t + 1 < NT and stage == 9:
            prefetch_x(t + 1)
            preworked.add(t + 1)
            ffn(1, cond, pre_tile=t + 1)
        else:
            ffn(1, cond)
        chk(8)
        S.tag = 'final'
        rs = norm_mod(0, 0, cond)
        for fc in range(8):
            stt(xT[:, fc, :], xT[:, fc, :], vec[:, 0, V_FING + fc:V_FING + fc + 1], rbuf[:, rs, :], ALU.mult, ALU.mult,
                [xT.r(fc), vec.r(), rbuf.r(rs)], [xT.r(fc)])
        for n in range(4):
            s = ocnt[0] % 2
            ocnt[0] += 1
            for half in range(2):
                pb = pnext()
                for j in range(4):
                    fc = half * 4 + j
                    tr(pb[:, j * 128:(j + 1) * 128], xT[:, fc, n * 128:(n + 1) * 128], identf, [xT.r(fc), identF.r()], [pb.r()])
                act(ost[:, s, half * 512:(half + 1) * 512], pb[:, :], AF.Copy, [pb.r()], [ost.r(s)])
            dma("sp", ti["yd"][ti["row0"] + n * 128:ti["row0"] + (n + 1) * 128, :], ost[:, s, :], [ost.r(s)], (), "yout")

    except StopBuild:
        pass
    stats = S.emit(nc, final_chans=["yout", "nst"])
    build.last_sched = S
    return nc, stats


def _consts():
    c = np.zeros((128, NCST), np.float32)
    j = np.arange(128, dtype=np.float32)[:, None]
    i = np.arange(128, dtype=np.float32)[None, :]
    c[:, C_ID:C_ID + 128] = np.eye(128, dtype=np.float32)
    c[:, C_NA:C_NA + 128] = -np.maximum(i - j, 0.0)
    c[:, C_NB:C_NB + 128] = -np.maximum(j - i, 0.0)
    c[:, C_M1:C_M1 + 128] = (i >= j)
    c[:, C_M2:C_M2 + 128] = (j >= i)
    c[:, C_NI1:C_NI1 + 128] = -(i + 1.0)
    c[:, C_NI2:C_NI2 + 128] = -(128.0 - i)
    c[:, C_PC] = -(127.0 - j[:, 0])
    c[:, C_PC + 1] = -j[:, 0]
    return c


def _rope_tables(ns_tiles):
    T = ns_tiles * 512
    pos = np.arange(T)
    row = (pos // 64).astype(np.float32)
    col = (pos % 64).astype(np.float32)
    freqs = (np.float32(10000.0) ** (-np.arange(32, dtype=np.float32) / np.float32(32))).astype(np.float32)
    ang = np.concatenate([row[:, None] * freqs, col[:, None] * freqs], axis=-1).astype(np.float32)
    cos = np.cos(ang).astype(np.float32)
    sin = np.sin(ang).astype(np.float32)
    tab = np.stack([cos, sin, -sin], axis=0)
    tab = tab.reshape(3, ns_tiles, 4, 128, 64).transpose(1, 3, 0, 2, 4)
    return np.ascontiguousarray(tab)


def _pack_weights(inp):
    st = weight_stream()
    mats = {k: np.asarray(inp[k]) for k in ("w_in_ab", "w_out_ab", "w_gate", "w_up", "w_down", "w_in_c", "w_out_c")}
    wall = np.empty((NUNIT, 128, 32, 128), np.float32)
    for p, (m, l, kc, cc) in enumerate(st):
        wall[p // 32, :, p % 32, :] = mats[m][l, kc * 128:(kc + 1) * 128, cc * 128:(cc + 1) * 128]
    return wall.reshape(NUNIT, 128, 4096)


def _shared_inputs(inp):
    f = lambda k: np.asarray(inp[k], np.float32)
    cols = [f("ada_b").reshape(96, 128).T]
    cols.append(None)
    cols.append(f("c_ctx").reshape(8, 128).T)
    cols.append(f("norm_mix_g").reshape(16, 128).T)
    cols.append(f("norm_ffn_g").reshape(16, 128).T)
    cols.append(f("ret_norm_g").reshape(4, 128).T)
    cols.append(f("conv_w").reshape(12, 128).T)
    cols.append(f("conv_b").reshape(4, 128).T)
    cols.append(f("final_norm_g").reshape(8, 128).T)
    rowv = np.concatenate([f("c_norm_g").reshape(-1), f("b_spatial").reshape(-1), f("ret_decay_logit").reshape(-1)])
    wsT = np.ascontiguousarray(f("w_spatial")[0].transpose(2, 0, 1).reshape(128, 512))
    return cols, rowv, wsT


_CACHE = {}


def kernel(**inp):
    NS, NP = 8, 2
    key = (NS, NP)
    if key not in _CACHE:
        _CACHE[key] = build(NS, NP)[0]
    nc = _CACHE[key]
    cols, rowv, wsT = _shared_inputs(inp)
    wall = _pack_weights(inp)
    cst = _consts()
    rope = _rope_tables(NS)
    adaw = np.ascontiguousarray(np.asarray(inp["ada_w"], np.float32))
    x_prompt = np.asarray(inp["x_prompt"], np.float32)
    x_sample = np.asarray(inp["x_sample"], np.float32)
    state_ret = np.asarray(inp["state_ret"], np.float32)
    c = np.asarray(inp["c"], np.float32)
    in_maps = []
    for i in range(8):
        cc = list(cols)
        cc[1] = c[i].reshape(8, 128).T
        in_maps.append({
            "xs": np.ascontiguousarray(x_sample[i]),
            "xp": np.ascontiguousarray(x_prompt[4 * i:4 * i + 4].reshape(1024, 1024)),
            "st0": np.ascontiguousarray(state_ret[i, 0]),
            "wall": wall, "adaw": adaw,
            "vecT": np.ascontiguousarray(np.concatenate(cc, axis=1)),
            "rowv": rowv, "wsT": wsT, "cst": cst, "ropeT": rope,
        })
    res = run_bass_kernel_spmd(nc, in_maps, core_ids=list(range(8))).results
    y_prompt = np.concatenate([r["yp"].reshape(4, 256, 1024) for r in res], axis=0)
    y_sample = np.stack([r["ys"] for r in res], axis=0)
    nstate = np.concatenate([r["nst"].reshape(4, 1, 2, 4, 128, 128) for r in res], axis=0)
    return (y_prompt.astype(np.float32), y_sample.astype(np.float32), nstate.astype(np.float32))
```
